# Optimizing a Trainium2 kernel written in Bass

```python
import jax, jax.numpy as jnp
from jax import lax
import numpy as np

D_MODEL = 2048
BATCH = 2
SEQ = 16384
DEPTH = 1

D_MIX = D_MODEL
CONV_W = D_MIX // 2
CONV_GROUPS = 16
GLA_HEADS = 4
GLA_DV = D_MIX - CONV_W
HEAD_V = GLA_DV // GLA_HEADS
GLA_DK = GLA_DV // 2
HEAD_K = GLA_DK // GLA_HEADS
GATE_RANK = 16
GATE_TAU = 16
CHUNK = 64
CONV_K = 3
D_FF = 5632
EPS = 1e-6
D_IN = 3 * CONV_W + 2 * GLA_DK + 2 * GLA_DV + GATE_RANK

kernel_name = "hybrid_conv_gla_convffn_adaln"


def rmsnorm(x, w):
    x32 = x.astype(jnp.float32)
    y = x32 * lax.rsqrt(jnp.mean(x32 * x32, axis=-1, keepdims=True) + EPS)
    return (y * w.astype(jnp.float32)).astype(x.dtype)


def causal_dwconv3(u, w):
    s = u.shape[1]
    up = jnp.pad(u, ((0, 0), (CONV_K - 1, 0), (0, 0)))
    return up[:, :s] * w[0] + up[:, 1:s + 1] * w[1] + up[:, 2:] * w[2]


def gla_chunked(q, k, v, log_a):
    bsz, s, h, dk = q.shape
    dv = v.shape[-1]
    nc = s // CHUNK

    def to_chunks(t):
        return t.reshape(bsz, nc, CHUNK, h, t.shape[-1]).transpose(1, 0, 3, 2, 4).astype(jnp.float32)

    mask = jnp.tril(jnp.ones((CHUNK, CHUNK), dtype=bool))[None, None, :, :, None]

    def step(state, inp):
        qc, kc, vc, gc = inp
        b = jnp.cumsum(gc, axis=2)
        diff = b[:, :, :, None, :] - b[:, :, None, :, :]
        decay = jnp.exp(jnp.where(mask, diff, -jnp.inf))
        scores = jnp.einsum('bhid,bhjd,bhijd->bhij', qc, kc, decay)
        o = jnp.einsum('bhij,bhje->bhie', scores, vc) + jnp.einsum('bhid,bhde->bhie', qc * jnp.exp(b), state)
        b_last = b[:, :, -1:, :]
        state = jnp.exp(b_last[:, :, 0, :])[..., None] * state + jnp.einsum(
            'bhjd,bhje->bhde', kc * jnp.exp(b_last - b), vc)
        return state, o

    state0 = jnp.zeros((bsz, h, dk, dv), jnp.float32)
    _, o = lax.scan(step, state0, (to_chunks(q), to_chunks(k), to_chunks(v), to_chunks(log_a)))
    return o.transpose(1, 0, 3, 2, 4).reshape(bsz, s, h, dv)


def hybrid_mixer(h, w_in, conv_w, gate_w2, gate_b, gla_norm_w, w_out):
    bsz, s, _ = h.shape
    proj = h @ w_in
    splits = np.cumsum([CONV_W, CONV_W, CONV_W, GLA_DK, GLA_DK, GLA_DV, GLA_DV])
    cb, cc, cx, q, k, v, r, a_lr = jnp.split(proj, splits, axis=-1)
    y_conv = cb * causal_dwconv3(cc * cx, conv_w)
    log_a = jax.nn.log_sigmoid((a_lr @ gate_w2 + gate_b).astype(jnp.float32)) / GATE_TAU
    q = q.reshape(bsz, s, GLA_HEADS, HEAD_K) * (HEAD_K ** -0.5)
    k = k.reshape(bsz, s, GLA_HEADS, HEAD_K)
    v = v.reshape(bsz, s, GLA_HEADS, HEAD_V)
    log_a = log_a.reshape(bsz, s, GLA_HEADS, HEAD_K)
    o = gla_chunked(q, k, v, log_a).astype(h.dtype)
    o = rmsnorm(o, gla_norm_w) * jax.nn.silu(r.reshape(bsz, s, GLA_HEADS, HEAD_V))
    y_gla = o.reshape(bsz, s, GLA_DV)
    return jnp.concatenate([y_conv, y_gla], axis=-1) @ w_out


def conv_ffn(h, w_up, ffn_conv_w, w_down):
    u = causal_dwconv3(h @ w_up, ffn_conv_w)
    g, val = jnp.split(u, 2, axis=-1)
    return (jax.nn.silu(g) * val) @ w_down


def setup_inputs(seed: int = 0) -> dict:
    key = jax.random.key(seed)
    ks = jax.random.split(key, 20)
    n = jax.random.normal
    f = jnp.float32
    L = DEPTH
    return {
        "x": n(ks[0], (BATCH, SEQ, D_MODEL), f),
        "c": n(ks[1], (BATCH, D_MODEL), f),
        "w_mod": n(ks[2], (L, D_MODEL, 6 * D_MODEL), f) * (0.5 * D_MODEL ** -0.5),
        "b_mod": 0.01 * n(ks[3], (L, 6 * D_MODEL), f),
        "mix_pre_w": 1.0 + 0.05 * n(ks[4], (L, D_MODEL), f),
        "mix_post_w": 1.0 + 0.05 * n(ks[5], (L, D_MODEL), f),
        "w_in": n(ks[6], (L, D_MODEL, D_IN), f) * D_MODEL ** -0.5,
        "conv_w": n(ks[7], (L, CONV_K, CONV_W), f) * CONV_K ** -0.5,
        "gate_w2": n(ks[8], (L, GATE_RANK, GLA_DK), f) * GATE_RANK ** -0.5,
        "gate_b": 0.01 * n(ks[9], (L, GLA_DK), f),
        "gla_norm_w": 1.0 + 0.05 * n(ks[10], (L, HEAD_V), f),
        "w_out": n(ks[11], (L, D_MIX, D_MODEL), f) * D_MIX ** -0.5,
        "ffn_pre_w": 1.0 + 0.05 * n(ks[12], (L, D_MODEL), f),
        "ffn_post_w": 1.0 + 0.05 * n(ks[13], (L, D_MODEL), f),
        "w_up": n(ks[14], (L, D_MODEL, 2 * D_FF), f) * D_MODEL ** -0.5,
        "ffn_conv_w": n(ks[15], (L, CONV_K, 2 * D_FF), f) * CONV_K ** -0.5,
        "w_down": n(ks[16], (L, D_FF, D_MODEL), f) * D_FF ** -0.5,
    }


def reference(x, c, w_mod, b_mod, mix_pre_w, mix_post_w, w_in, conv_w, gate_w2, gate_b, gla_norm_w,
              w_out, ffn_pre_w, ffn_post_w, w_up, ffn_conv_w, w_down):
    c_act = jax.nn.silu(c)
    for l in range(DEPTH):
        mod = (c_act @ w_mod[l] + b_mod[l])[:, None, :]
        sh_m, sc_m, g_m, sh_f, sc_f, g_f = jnp.split(mod, 6, axis=-1)
        h = rmsnorm(x, mix_pre_w[l]) * (1 + sc_m) + sh_m
        y = hybrid_mixer(h, w_in[l], conv_w[l], gate_w2[l], gate_b[l], gla_norm_w[l], w_out[l])
        x = x + g_m * rmsnorm(y, mix_post_w[l])
        h = rmsnorm(x, ffn_pre_w[l]) * (1 + sc_f) + sh_f
        y = conv_ffn(h, w_up[l], ffn_conv_w[l], w_down[l])
        x = x + g_f * rmsnorm(y, ffn_post_w[l])
    return x
```

```python
import numpy as np
import concourse.bass as bass
import concourse.mybir as mybir
from concourse.bass_utils import run_bass_kernel_spmd

F32 = mybir.dt.float32
BF16 = mybir.dt.bfloat16
AF = mybir.ActivationFunctionType
ALU = mybir.AluOpType

D = 2048
KC = 16
DIN = 6160
DFF = 5632
NUP = 11264
EPS = 1e-6


class Tok:
    __slots__ = ("sem", "val", "vc")

    def __init__(self, sem, val, vc):
        self.sem = sem
        self.val = val
        self.vc = vc


class Buf:
    __slots__ = ("name", "w", "r")

    def __init__(self, name):
        self.name = name
        self.w = None
        self.r = []


class Eng:
    def __init__(self, name, key):
        self.name = name
        self.key = key
        self.cnt = 0
        self.vc = {}
        self.ops = []


class Sched:
    ENGS = ("pe", "act", "dve", "pool", "sp")

    def __init__(self, nc):
        self.nc = nc
        self.sems = []
        self.engs = {}
        for n in self.ENGS:
            k = self._new_sem("prog_" + n)
            self.engs[n] = Eng(n, k)
        self.dma_sem_val = {}
        self.n_wait = 0
        self.n_ins = 0

    def _new_sem(self, name):
        h = self.nc.alloc_semaphore(name)
        self.sems.append(h)
        return len(self.sems) - 1

    def new_dma_sem(self, name):
        k = self._new_sem(name)
        self.dma_sem_val[k] = 0
        return k

    def _collect(self, reads, writes, extra):
        deps = []
        for b in reads:
            if b.w is not None:
                deps.append(b.w)
        for b in writes:
            if b.w is not None:
                deps.append(b.w)
            deps.extend(b.r)
        if extra:
            deps.extend(extra)
        return deps

    def _waits(self, eng, deps):
        best = {}
        evc = eng.vc
        for t in deps:
            if evc.get(t.sem, 0) >= t.val:
                continue
            b = best.get(t.sem)
            if b is None or b.val < t.val:
                best[t.sem] = t
        if not best:
            return []
        cand = list(best.values())
        chosen = []
        for t in cand:
            implied = False
            for t2 in cand:
                if t2 is not t and t2.vc.get(t.sem, 0) >= t.val:
                    implied = True
                    break
            if not implied:
                chosen.append((t.sem, t.val))
        for t in cand:
            if evc.get(t.sem, 0) < t.val:
                evc[t.sem] = t.val
            for k, v in t.vc.items():
                if evc.get(k, 0) < v:
                    evc[k] = v
        return chosen

    def op(self, engname, fn, reads=(), writes=(), extra=None, same_ok=False):
        eng = self.engs[engname]
        deps = self._collect(reads, writes, extra)
        if same_ok:
            deps = [t for t in deps if t.sem != eng.key]
        waits = self._waits(eng, deps)
        self.n_wait += len(waits)
        self.n_ins += 1
        eng.cnt += 1
        val = eng.cnt
        tok = Tok(eng.key, val, dict(eng.vc))
        sems = self.sems
        key = eng.key

        def thunk(e):
            for k, v in waits:
                e.wait_ge(sems[k], v)
            ins = fn(e)
            ins.then_inc(sems[key], 1)
        eng.ops.append(thunk)
        for b in reads:
            b.r.append(tok)
        for b in writes:
            b.w = tok
            b.r = []
        return tok

    def dma(self, engname, semkey, fn, reads=(), writes=(), extra=None):
        eng = self.engs[engname]
        deps = self._collect(reads, writes, extra)
        waits = self._waits(eng, deps)
        self.n_wait += len(waits)
        self.n_ins += 1
        self.dma_sem_val[semkey] += 16
        val = self.dma_sem_val[semkey]
        tok = Tok(semkey, val, dict(eng.vc))
        sems = self.sems

        def thunk(e):
            for k, v in waits:
                e.wait_ge(sems[k], v)
            fn(e).then_inc(sems[semkey], 16)
        eng.ops.append(thunk)
        for b in reads:
            b.r.append(tok)
        for b in writes:
            b.w = tok
            b.r = []
        return tok

    def final_wait(self, engname, toks):
        eng = self.engs[engname]
        waits = self._waits(eng, toks)
        sems = self.sems

        def thunk(e):
            for k, v in waits:
                e.wait_ge(sems[k], v)
        eng.ops.append(thunk)

    def emit(self):
        nc = self.nc
        engs = self.engs
        with nc.Block() as block:
            @block.tensor
            def _(e):
                for t in engs["pe"].ops:
                    t(e)

            @block.scalar
            def _(e):
                for t in engs["act"].ops:
                    t(e)

            @block.vector
            def _(e):
                for t in engs["dve"].ops:
                    t(e)

            @block.gpsimd
            def _(e):
                for t in engs["pool"].ops:
                    t(e)

            @block.sync
            def _(e):
                for t in engs["sp"].ops:
                    t(e)


def alias_fence(from_bufs, to_bufs):
    toks = []
    for b in from_bufs:
        if b.w is not None:
            toks.append(b.w)
        toks.extend(b.r)
    best = {}
    for t in toks:
        o = best.get(t.sem)
        if o is None or o.val < t.val:
            best[t.sem] = t
    toks = list(best.values())
    for b in to_bufs:
        b.r.extend(toks)


class Rot:
    def __init__(self, items):
        self.items = list(items)
        self.i = 0

    def next(self):
        r = self.items[self.i % len(self.items)]
        self.i += 1
        return r


def build_program(NMAIN, NPRE, dbg=False):
    NT = NPRE + 1 + NMAIN
    nc = bass.Bass("TRN2", target_bir_lowering=False)
    S = Sched(nc)

    def din(name, shape):
        return nc.dram_tensor(name, list(shape), F32, kind="ExternalInput").ap()

    xmain = din("xmain", [NMAIN * 512, D])
    xpre = din("xpre", [NPRE * 512, D])
    tmask_d = din("tmask", [128, NT])
    cvec = din("cvec", [16, 128])
    w_mod = din("w_mod", [D, 6 * D])
    b_mod = din("b_mod", [96, 128])
    pre_m = din("pre_m", [16, 128])
    post_m = din("post_m", [16, 128])
    w_in = din("w_in", [D, DIN])
    conv_w = din("conv_w", [24, 128])
    gate_w2 = din("gate_w2", [16, 512])
    gate_b = din("gate_b", [1, 512])
    gnw = din("gnw", [2, 128])
    w_out = din("w_out", [D, D])
    pre_f = din("pre_f", [16, 128])
    post_f = din("post_f", [16, 128])
    w_up = din("w_up", [D, NUP])
    fconv_w = din("fconv_w", [264, 128])
    w_down = din("w_down", [DFF, D])
    out_d = nc.dram_tensor("out", [NMAIN * 512, D], F32, kind="ExternalOutput").ap()

    s_win = nc.dram_tensor("s_win", [12, 128, 16, 512], BF16, kind="Internal").ap()
    s_wout = nc.dram_tensor("s_wout", [4, 128, 16, 512], BF16, kind="Internal").ap()
    s_wup = nc.dram_tensor("s_wup", [22, 128, 16, 512], BF16, kind="Internal").ap()
    s_wdn = nc.dram_tensor("s_wdn", [12, 128, 16, 512], BF16, kind="Internal").ap()

    sb_off = [((nc.sbuf_base + 63) // 64) * 64]
    sb_top = nc.sbuf_top

    def dsize(dt):
        return 4 if dt == F32 else 2

    def salloc(name, shape, dt, at=None):
        n = dsize(dt)
        for v in shape[1:]:
            n *= v
        if at is None:
            at = sb_off[0]
            sb_off[0] = ((at + n + 63) // 64) * 64
        assert at + n <= sb_top, (name, at, n, sb_top)
        return nc.alloc_sbuf_tensor_at(name, list(shape), dt, offset=at)

    X = salloc("X", [128, 4, D], F32)
    hT = salloc("hT", [128, 16, 512], BF16)
    Wsl = [salloc(f"W{i}", [128, 16, 512], BF16) for i in range(3)]
    AR = sb_off[0]
    ARENA = 79936
    sb_off[0] = AR + ARENA
    yT = salloc("yT", [128, 16, 512], BF16, at=AR + 0)
    vtm = salloc("vtm", [128, 4, 1024], BF16, at=AR + 16384)
    rs = salloc("rs", [128, 4, 1024], BF16, at=AR + 24576)
    qT = salloc("qT", [128, 4, 512], BF16, at=AR + 32768)
    kT = salloc("kT", [128, 4, 512], BF16, at=AR + 36864)
    ktm = salloc("ktm", [128, 4, 512], BF16, at=AR + 40960)
    cbs = salloc("cbs", [128, 4, 512], BF16, at=AR + 45056)
    ccs = salloc("ccs", [128, 4, 512], BF16, at=AR + 49152)
    sp = salloc("sp", [128, 2, 512], F32, at=AR + 53248)
    etmp = salloc("etmp", [128, 2, 512], F32, at=AR + 57344)
    Etmp = salloc("Etmp", [128, 2, 512], F32, at=AR + 61440)
    ubuf = salloc("ubuf", [128, 2, 514], F32, at=AR + 65536)
    cacc = salloc("cacc", [128, 2, 512], F32, at=AR + 69696)
    sTb = salloc("sTb", [128, 2, 4, 128], BF16, at=AR + 73792)
    ygt = salloc("ygt", [128, 2, 1024], BF16, at=AR + 75840)
    ymix = salloc("ymix", [128, 4, D], F32, at=AR + 16384)
    hid = salloc("hid", [128, 44, 512], BF16, at=AR + 0)
    sg = salloc("sg", [128, 4, 512], F32, at=AR + 45056)
    facc = salloc("facc", [128, 2, 512], F32, at=AR + 53248)
    yffn = salloc("yffn", [128, 4, D], F32, at=AR + 45056)
    vstage = salloc("vstage", [128, 128], F32, at=AR + 0)
    dtmp = salloc("dtmp", [128, 2, 128], F32, at=AR + 512)
    Gm = salloc("Gm", [128, D], BF16)
    Gf = salloc("Gf", [128, D], BF16)
    hn = salloc("hn", [128, 2, D], BF16)
    Sst = salloc("Sst", [128, 4, 256], F32)
    Sbf = salloc("Sbf", [128, 4, 256], BF16)
    gw2aug = salloc("gw2aug", [128, 512], F32)
    alrT = salloc("alrT", [128, 512], F32)
    identf = salloc("identf", [128, 128], F32)
    identb = salloc("identb", [128, 128], BF16)
    tri = salloc("tri", [128, 128], F32)
    ones = salloc("ones", [128, 128], F32)
    Walr = salloc("Walr", [128, 16, 16], BF16)
    bmod_fm = salloc("bmod_fm", [128, 96], F32)
    mod_fm = salloc("mod_fm", [128, 96], F32)
    prem_fm = salloc("prem_fm", [128, 16], F32)
    postm_fm = salloc("postm_fm", [128, 16], F32)
    pref_fm = salloc("pref_fm", [128, 16], F32)
    postf_fm = salloc("postf_fm", [128, 16], F32)
    A_m = salloc("A_m", [128, 16], F32)
    A_f = salloc("A_f", [128, 16], F32)
    Gm_fm = salloc("Gm_fm", [128, 16], F32)
    Gf_fm = salloc("Gf_fm", [128, 16], F32)
    convw_fm = salloc("convw_fm", [128, 24], F32)
    fconvw_fm = salloc("fconvw_fm", [128, 264], F32)
    gnw_fm = salloc("gnw_fm", [128, 2], F32)
    c_fm = salloc("c_fm", [128, 16], F32)
    ca_bf = salloc("ca_bf", [128, 16], BF16)
    tm = salloc("tm", [128, NT], F32)
    ssb = salloc("ssb", [128, 4], F32)
    rstd = salloc("rstd", [128, 4], F32)
    ssq_o = salloc("ssq_o", [128, 2, 4], F32)
    rstd_o = salloc("rstd_o", [128, 2, 4], F32)
    ebt = salloc("ebt", [128, 4, 4], F32)
    uhalo = salloc("uhalo", [128, 8, 2], F32)
    fhalo = salloc("fhalo", [128, 88, 2], F32)
    epsb = salloc("epsb", [128, 1], F32)
    jk = salloc("jk", [128, 512], BF16)

    pp = [nc.alloc_psum_tensor(f"pp{j}", [128, 1024], F32) for j in range(4)]

    class Bank:
        def __init__(self, i):
            self.i = i
            self.f = pp[i // 2][:, (i % 2) * 512:(i % 2 + 1) * 512]
            self.b = self.f.bitcast(BF16)
            self.buf = Buf(f"bank{i}")
    bk = [Bank(i) for i in range(8)]

    def pair_f(j):
        return pp[j][:, :]

    def pair_b(j):
        return pp[j][:, :].bitcast(BF16)

    bX = [Buf(f"X{s}") for s in range(4)]
    b_hT = [[Buf(f"hT{kc}_{s}") for s in range(4)] for kc in range(16)]
    b_hT_all = [b for row in b_hT for b in row]
    bW = [Buf(f"Wslot{i}") for i in range(3)]
    b_hn = [Buf("hn0"), Buf("hn1")]
    b_ss = Buf("ss")
    b_rstd = Buf("rstd")
    b_const = Buf("const")
    b_alrT = Buf("alrT")
    b_sp = [Buf("sp0"), Buf("sp1")]
    b_etmp = [Buf("et0"), Buf("et1")]
    b_Etmp = [Buf("E0"), Buf("E1")]
    b_qT = [Buf(f"qT{h}") for h in range(4)]
    b_kT = [Buf(f"kT{h}") for h in range(4)]
    b_ktm = Buf("ktm")
    b_vtm = [Buf(f"vtm{s}") for s in range(4)]
    b_rs = [Buf(f"rs{s}") for s in range(4)]
    b_ebt = Buf("ebt")
    b_sT = [Buf("sT0"), Buf("sT1")]
    b_ygt = [Buf("ygt0"), Buf("ygt1")]
    b_ssqo = [Buf("ssqo0"), Buf("ssqo1")]
    b_rstdo = [Buf("rstdo0"), Buf("rstdo1")]
    b_S = Buf("S")
    b_Sbf = Buf("Sbf")
    b_cbs = [Buf(f"cbs{i}") for i in range(4)]
    b_ccs = [Buf(f"ccs{i}") for i in range(4)]
    b_u = [Buf("u0"), Buf("u1")]
    b_cacc = [Buf("cacc0"), Buf("cacc1")]
    b_uhalo = [Buf(f"uhalo{c}") for c in range(8)]
    b_fhalo = [Buf(f"fhalo{c}") for c in range(88)]
    b_yT = [[Buf(f"yT{c}_{s}") for s in range(4)] for c in range(16)]
    b_ymix = [Buf(f"ymix{s}") for s in range(4)]
    b_hid = [Buf(f"hid{c}") for c in range(44)]
    b_sg = [Buf(f"sg{i}") for i in range(4)]
    b_facc = [Buf("facc0"), Buf("facc1")]
    b_yffn = [Buf(f"yffn{s}") for s in range(4)]
    b_out = [Buf(f"out{s}") for s in range(4)]
    mix_bufs = (b_sp + b_etmp + b_Etmp + b_qT + b_kT + [b_ktm] + b_vtm + b_rs + b_sT + b_ygt + b_cbs + b_ccs
                + b_u + b_cacc + [b for row in b_yT for b in row] + b_ymix)
    ffn_bufs = b_hid + b_sg + b_facc + b_yffn
    tmp_bufs = (b_sp + b_etmp + b_Etmp + b_qT + b_kT + [b_ktm] + b_vtm + b_rs + b_sT + b_ygt + b_cbs + b_ccs
                + b_u + b_cacc)

    sem_W = [S.new_dma_sem(f"semW{i}") for i in range(3)]
    sem_x = [S.new_dma_sem(f"semx{i}") for i in range(4)]
    sem_o = [S.new_dma_sem(f"semo{i}") for i in range(4)]
    sem_cv = [S.new_dma_sem(f"semcv{i}") for i in range(4)]
    sem_misc = S.new_dma_sem("semmisc")
    sem_misc2 = S.new_dma_sem("semmisc2")
    sem_m3 = S.new_dma_sem("semm3")
    sem_m4 = S.new_dma_sem("semm4")
    sem_m5 = S.new_dma_sem("semm5")

    def mm_group(out_ap, pairs, reads, writes, same_ok=False):
        def fn(e):
            r = None
            n = len(pairs)
            for i, (l, rh) in enumerate(pairs):
                r = e.matmul(out_ap, lhsT=l, rhs=rh, start=(i == 0), stop=(i == n - 1))
            return r
        return S.op("pe", fn, reads=reads, writes=writes, same_ok=same_ok)

    def act(out, in_, func, reads, writes, scale=1.0, bias=None, accum=None):
        kw = {}
        if bias is not None:
            kw["bias"] = bias
        if accum is not None:
            kw["accum_out"] = accum
        return S.op("act", lambda e: e.activation(out=out, in_=in_, func=func, scale=scale, **kw),
                    reads=reads, writes=writes)

    misc_bufs = Buf("vstage")
    S.op("pool", lambda e: e.memset(identf[:], 1.0), writes=[b_const])
    S.op("pool", lambda e: e.affine_select(out=identf[:], in_=identf[:], pattern=[[-1, 128]],
                                           compare_op=ALU.is_equal, fill=0.0, base=0, channel_multiplier=1),
         writes=[b_const])
    S.op("pool", lambda e: e.tensor_copy(out=identb[:], in_=identf[:]), reads=[b_const], writes=[Buf("identb")])
    b_tri = Buf("tri")
    S.op("pool", lambda e: e.memset(tri[:], 1.0), writes=[b_tri])
    S.op("pool", lambda e: e.affine_select(out=tri[:], in_=tri[:], pattern=[[1, 128]],
                                           compare_op=ALU.is_ge, fill=0.0, base=0, channel_multiplier=-1),
         writes=[b_tri])
    S.op("pool", lambda e: e.memset(ones[:], 1.0), writes=[b_const])
    S.op("pool", lambda e: e.memset(epsb[:], EPS), writes=[b_const])
    S.op("pool", lambda e: e.memset(Sst[:], 0.0), writes=[b_S])
    S.op("pool", lambda e: e.memset(Sbf[:], 0.0), writes=[b_Sbf])
    S.op("pool", lambda e: e.memset(uhalo[:], 0.0), writes=b_uhalo)
    S.op("pool", lambda e: e.memset(fhalo[:], 0.0), writes=b_fhalo)
    S.op("pool", lambda e: e.memset(gw2aug[:], 0.0), writes=[b_const])
    S.op("pool", lambda e: e.memset(alrT[:], 1.0), writes=[b_alrT])
    S.dma("sp", sem_m3, lambda e: e.dma_start(out=tm[:], in_=tmask_d), writes=[b_const])
    S.dma("sp", sem_m4, lambda e: e.dma_start(out=gw2aug[0:16, :], in_=gate_w2), writes=[b_const])
    S.dma("sp", sem_m5, lambda e: e.dma_start(out=gw2aug[16:17, :], in_=gate_b), writes=[b_const])
    S.dma("pool", sem_misc2, lambda e: e.dma_start(out=Walr[:], in_=w_in[:, 6144:6160].rearrange("(kc p) n -> p kc n", p=128)),
          writes=[b_const])

    b_vst = Buf("vstage")
    vrot = Rot([bk[6], bk[7]])

    def vec_fm(src, n, dst):
        S.dma("sp", sem_misc, lambda e: e.dma_start(out=vstage[0:n, :], in_=src), writes=[b_vst])
        pb = vrot.next()
        S.op("pe", lambda e: e.transpose(out=pb.f[:, 0:n], in_=vstage[0:n, :], identity=identf[0:n, 0:n]),
             reads=[b_vst, b_const], writes=[pb.buf])
        S.op("dve", lambda e: e.tensor_copy(out=dst, in_=pb.f[:, 0:n]), reads=[pb.buf], writes=[b_const])

    vec_fm(cvec, 16, c_fm[:, :])
    vec_fm(b_mod, 96, bmod_fm[:, :])
    vec_fm(pre_m, 16, prem_fm[:, :])
    vec_fm(post_m, 16, postm_fm[:, :])
    vec_fm(pre_f, 16, pref_fm[:, :])
    vec_fm(post_f, 16, postf_fm[:, :])
    vec_fm(conv_w, 24, convw_fm[:, :])
    vec_fm(gnw, 2, gnw_fm[:, :])
    for t3 in range(3):
        vec_fm(fconv_w[t3 * 88:(t3 + 1) * 88, :], 88, fconvw_fm[:, t3 * 88:(t3 + 1) * 88])
    act(ca_bf[:], c_fm[:], AF.Silu, reads=[b_const], writes=[b_const])

    pmod = bk[5]
    for blk in range(24):
        sl = blk % 3
        S.dma("pool", sem_W[sl], lambda e, sl=sl, blk=blk: e.dma_start(
            out=Wsl[sl][:], in_=w_mod[:, blk * 512:(blk + 1) * 512].rearrange("(kc p) n -> p kc n", p=128)),
            writes=[bW[sl]])
        for nch in range(4):
            col = blk * 4 + nch
            mm_group(pmod.f[:, col:col + 1],
                     [(Wsl[sl][:, kc, nch * 128:(nch + 1) * 128], ca_bf[:, kc:kc + 1]) for kc in range(16)],
                     reads=[bW[sl], b_const], writes=[pmod.buf], same_ok=True)
    S.op("dve", lambda e: e.tensor_tensor(out=mod_fm[:], in0=pmod.f[:, 0:96], in1=bmod_fm[:], op=ALU.add),
         reads=[pmod.buf, b_const], writes=[b_const])
    S.op("dve", lambda e: e.scalar_tensor_tensor(out=A_m[:], in0=mod_fm[:, 16:32], scalar=1.0, in1=prem_fm[:],
                                                 op0=ALU.add, op1=ALU.mult), reads=[b_const], writes=[b_const])
    S.op("dve", lambda e: e.scalar_tensor_tensor(out=A_f[:], in0=mod_fm[:, 64:80], scalar=1.0, in1=pref_fm[:],
                                                 op0=ALU.add, op1=ALU.mult), reads=[b_const], writes=[b_const])
    S.op("dve", lambda e: e.tensor_tensor(out=Gm_fm[:], in0=mod_fm[:, 32:48], in1=postm_fm[:], op=ALU.mult),
         reads=[b_const], writes=[b_const])
    S.op("dve", lambda e: e.tensor_tensor(out=Gf_fm[:], in0=mod_fm[:, 80:96], in1=postf_fm[:], op=ALU.mult),
         reads=[b_const], writes=[b_const])
    B_m = mod_fm[:, 0:16]
    B_f = mod_fm[:, 48:64]
    b_dtmp = [Buf("dtmp0"), Buf("dtmp1")]
    grot = Rot([bk[6], bk[7]])
    for (gfm, gdst) in ((Gm_fm, Gm), (Gf_fm, Gf)):
        for g4 in range(4):
            pb = grot.next()
            for j in range(4):
                c = g4 * 4 + j
                dd = c % 2
                S.op("dve", lambda e, c=c, dd=dd, gfm=gfm: e.tensor_scalar(
                    out=dtmp[:, dd, :], in0=identf[:], scalar1=gfm[:, c:c + 1], scalar2=None, op0=ALU.mult),
                    reads=[b_const], writes=[b_dtmp[dd]])
                mm_group(pb.f[:, j * 128:(j + 1) * 128], [(ones[:], dtmp[:, dd, :])],
                         reads=[b_const, b_dtmp[dd]], writes=[pb.buf], same_ok=True)
            S.op("dve", lambda e, pb=pb, g4=g4, gdst=gdst: e.tensor_copy(out=gdst[:, g4 * 512:(g4 + 1) * 512], in_=pb.f),
                 reads=[pb.buf], writes=[b_const])

    cv_toks = []
    scr_buf = {}

    def convert(key, dst, src):
        i = len(cv_toks)
        b = Buf("scr")
        extra = [cv_toks[i - 4]] if i >= 4 else None
        t = S.dma("pool", sem_cv[i % 4], lambda e: e.dma_start(out=dst, in_=src), writes=[b], extra=extra)
        cv_toks.append(t)
        scr_buf[key] = b

    def win_src(blk):
        return w_in[:, blk * 512:(blk + 1) * 512].rearrange("(kc p) n -> p kc n", p=128)

    for blk in (7, 8, 9, 6, 10, 11, 0, 2, 4, 1, 3, 5):
        convert(("win", blk), s_win[blk], win_src(blk))
    for blk in range(4):
        convert(("wout", blk), s_wout[blk], w_out[:, blk * 512:(blk + 1) * 512].rearrange("(kc p) n -> p kc n", p=128))
    for b in range(11):
        for blk in (b, 11 + b):
            convert(("wup", blk), s_wup[blk], w_up[:, blk * 512:(blk + 1) * 512].rearrange("(kc p) n -> p kc n", p=128))
    for cg in range(4):
        for kb in range(3):
            nk = 16 if kb < 2 else 12
            convert(("wdn", cg * 3 + kb), s_wdn[cg * 3 + kb][:, 0:nk, :],
                    w_down[kb * 2048:kb * 2048 + nk * 128, cg * 512:(cg + 1) * 512].rearrange("(kc p) n -> p kc n", p=128))

    tiles = []
    for i in range(NPRE):
        ns = 4 if i < NPRE - 1 else 3
        tiles.append(("pre", xpre[i * 512:i * 512 + ns * 128, :], ns, i, None))
    tiles.append(("warm", xpre[NPRE * 512 - 128:NPRE * 512, :], 1, NPRE, None))
    for i in range(NMAIN):
        tiles.append(("main", xmain[i * 512:(i + 1) * 512, :], 4, NPRE + 1 + i, out_d[i * 512:(i + 1) * 512, :]))

    plan = []
    for (kind, _, _, _, _) in tiles:
        if kind == "pre":
            seq = [("win", 7), ("win", 8), ("win", 9)]
        else:
            seq = [("win", b) for b in (7, 6, 8, 9, 10, 11, 0, 2, 4, 1, 3, 5)]
            seq += [("wout", b) for b in range(4)]
            for b in range(11):
                seq += [("wup", b), ("wup", 11 + b)]
            if kind == "main":
                seq += [("wdn", i) for i in range(12)]
        plan.extend(seq)
    scr_ap = {"win": s_win, "wout": s_wout, "wup": s_wup, "wdn": s_wdn}
    wst = {"issued": 0, "next": 0}

    def w_issue_upto(n):
        while wst["issued"] < min(n, len(plan)):
            i = wst["issued"]
            sl = i % 3
            key = plan[i]
            nk = 12 if (key[0] == "wdn" and key[1] % 3 == 2) else 16
            src = scr_ap[key[0]][key[1]][:, 0:nk, :]
            S.dma("sp", sem_W[sl], lambda e, sl=sl, nk=nk, src=src: e.dma_start(out=Wsl[sl][:, 0:nk, :], in_=src),
                  reads=[scr_buf[key]], writes=[bW[sl]])
            wst["issued"] += 1

    def w_get(expect):
        i = wst["next"]
        assert plan[i] == expect, (i, plan[i], expect)
        w_issue_upto(i + 3)
        wst["next"] += 1
        return Wsl[i % 3], bW[i % 3]

    allbanks = Rot(bk)

    def stage_in(xsrc, ns, A, B, from_X):
        for s in range(ns):
            if not from_X:
                S.dma("sp", sem_x[s], lambda e, s=s: e.dma_start(out=X[:, s, :], in_=xsrc[s * 128:(s + 1) * 128, :]),
                      writes=[bX[s]])
            act(hn[:, (s + 1) % 2, :], X[:, s, :], AF.Square, reads=[bX[s]], writes=[b_hn[(s + 1) % 2], b_ss],
                accum=ssb[:, s:s + 1])
        act(rstd[:, 0:ns], ssb[:, 0:ns], AF.Ln, reads=[b_ss], writes=[b_rstd], scale=1.0 / D, bias=epsb[:])
        act(rstd[:, 0:ns], rstd[:, 0:ns], AF.Exp, reads=[b_rstd], writes=[b_rstd], scale=-0.5)
        trot = Rot([bk[0], bk[1], bk[2], bk[3]])
        for s in range(ns):
            hb = s % 2
            S.op("dve", lambda e, s=s, hb=hb: e.tensor_scalar(out=hn[:, hb, :], in0=X[:, s, :], scalar1=rstd[:, s:s + 1],
                                                             scalar2=None, op0=ALU.mult),
                 reads=[bX[s], b_rstd], writes=[b_hn[hb]])
            for g in range(4):
                pb = trot.next()
                pv = pb.b[:, 0:512].rearrange("p (a b) -> p a b", a=4)
                for j in range(4):
                    kc = g * 4 + j
                    S.op("pe", lambda e, kc=kc, j=j, hb=hb, pv=pv: e.transpose(
                        out=pv[:, j, :], in_=hn[:, hb, kc * 128:(kc + 1) * 128], identity=identb[:]),
                        reads=[b_hn[hb]], writes=[pb.buf], same_ok=True)
                for j in range(4):
                    kc = g * 4 + j
                    S.op("dve", lambda e, kc=kc, j=j, s=s, pv=pv: e.tensor_scalar(
                        out=hT[:, kc, s * 128:(s + 1) * 128], in0=pv[:, j, :], scalar1=A[:, kc:kc + 1],
                        scalar2=B[:, kc:kc + 1], op0=ALU.mult, op1=ALU.add),
                        reads=[pb.buf, b_const], writes=[b_hT[kc][s]])

    def hT_reads(ns):
        return [b_hT[kc][s] for kc in range(16) for s in range(ns)]

    def fm_block(Wt, Wb, ns, i):
        N = ns * 128
        pb = allbanks.next()
        mm_group(pb.f[:, 0:N], [(Wt[:, kc, i * 128:(i + 1) * 128], hT[:, kc, 0:N]) for kc in range(16)],
                 reads=[Wb] + hT_reads(ns), writes=[pb.buf])
        return pb

    def tm_block(Wt, Wb, s, src, src_bufs, nk=16, kofs=0):
        pb = allbanks.next()
        mm_group(pb.f, [(src[:, kofs + kc, s * 128:(s + 1) * 128], Wt[:, kc, :]) for kc in range(nk)],
                 reads=[Wb] + src_bufs, writes=[pb.buf])
        return pb

    for (kind, xsrc, ns, mcol, odst) in tiles:
        N = ns * 128
        full = kind != "pre"
        tmc = tm[:, mcol:mcol + 1]
        alias_fence(ffn_bufs, mix_bufs)
        if mcol == 0:
            alias_fence([b_vst] + b_dtmp, mix_bufs)
        stage_in(xsrc, ns, A_m, B_m, from_X=False)

        pa = bk[4]
        mm_group(pa.f[0:16, 0:N], [(Walr[:, kc, :], hT[:, kc, 0:N]) for kc in range(16)],
                 reads=[b_const] + hT_reads(ns), writes=[pa.buf])
        act(alrT[0:16, 0:N], pa.f[0:16, 0:N], AF.Copy, reads=[pa.buf], writes=[b_alrT])
        pnb = [bk[0], bk[1], bk[2], bk[3]]
        zrot = Rot([bk[5], bk[6]])
        for s in range(ns):
            pz = zrot.next()
            mm_group(pz.f, [(alrT[0:32, s * 128:(s + 1) * 128], gw2aug[0:32, :])], reads=[b_alrT, b_const], writes=[pz.buf])
            sb_ = s % 2
            act(etmp[:, sb_, :], pz.f, AF.Exp, reads=[pz.buf], writes=[b_etmp[sb_]], scale=-1.0)
            act(sp[:, sb_, :], etmp[:, sb_, :], AF.Ln, reads=[b_etmp[sb_], b_const], writes=[b_sp[sb_]], scale=1.0, bias=ones[:, 0:1])
            for h in range(4):
                mm_group(pnb[h].f[:, s * 128:(s + 1) * 128], [(sp[:, sb_, h * 128:(h + 1) * 128], tri[:])],
                         reads=[b_sp[sb_], b_tri], writes=[pnb[h].buf], same_ok=True)
        for h in range(4):
            act(ebt[:, h, 0:ns], pnb[h].f[:, 127:N:128], AF.Exp, reads=[pnb[h].buf], writes=[b_ebt], scale=-1.0 / 16)

        Wt, Wb = w_get(("win", 7))
        kq_rot = Rot([bk[4], bk[5]])
        pktv = pair_b(3).rearrange("p (s h d) -> p s h d", s=4, h=4)
        pkt_bufs = [bk[6].buf, bk[7].buf]
        for h in range(4):
            pb = kq_rot.next()
            mm_group(pb.f[:, 0:N], [(Wt[:, kc, h * 128:(h + 1) * 128], hT[:, kc, 0:N]) for kc in range(16)],
                     reads=[Wb] + hT_reads(ns), writes=[pb.buf])
            eb_ = h % 2
            act(Etmp[:, eb_, 0:N], pnb[h].f[:, 0:N], AF.Exp, reads=[pnb[h].buf], writes=[b_Etmp[eb_]], scale=1.0 / 16)
            S.op("dve", lambda e, h=h, pb=pb, eb_=eb_, N=N: e.tensor_tensor(out=kT[:, h, 0:N], in0=pb.f[:, 0:N], in1=Etmp[:, eb_, 0:N],
                                                                    op=ALU.mult),
                 reads=[pb.buf, b_Etmp[eb_]], writes=[b_kT[h]])
            for s in range(ns):
                S.op("pe", lambda e, h=h, s=s: e.transpose(out=pktv[:, s, h, :], in_=kT[:, h, s * 128:(s + 1) * 128],
                                                          identity=identb[:]),
                     reads=[b_kT[h]], writes=pkt_bufs, same_ok=True)
        S.op("act", lambda e, ns=ns: e.activation(out=ktm[:, 0:ns, :], in_=pair_b(3)[:, 0:ns * 512].rearrange("p (s n) -> p s n", s=ns),
                                           func=AF.Copy),
             reads=pkt_bufs, writes=[b_ktm])

        if full:
            Wt, Wb = w_get(("win", 6))
            for h in range(4):
                pb = kq_rot.next()
                mm_group(pb.f[:, 0:N], [(Wt[:, kc, h * 128:(h + 1) * 128], hT[:, kc, 0:N]) for kc in range(16)],
                         reads=[Wb] + hT_reads(ns), writes=[pb.buf])
                eb_ = h % 2
                act(Etmp[:, eb_, 0:N], pnb[h].f[:, 0:N], AF.Exp, reads=[pnb[h].buf], writes=[b_Etmp[eb_]], scale=-1.0 / 16)
                S.op("dve", lambda e, h=h, pb=pb, eb_=eb_, N=N: e.scalar_tensor_tensor(
                    out=qT[:, h, 0:N], in0=pb.f[:, 0:N], scalar=128.0 ** -0.5, in1=Etmp[:, eb_, 0:N],
                    op0=ALU.mult, op1=ALU.mult),
                    reads=[pb.buf, b_Etmp[eb_]], writes=[b_qT[h]])

        for half in range(2):
            Wt, Wb = w_get(("win", 8 + half))
            for s in range(ns):
                pb = tm_block(Wt, Wb, s, hT, [b_hT[kc][s] for kc in range(16)])
                act(vtm[:, s, half * 512:(half + 1) * 512], pb.f, AF.Identity, reads=[pb.buf, b_const], writes=[b_vtm[s]],
                    scale=tmc)
        if full:
            for half in range(2):
                Wt, Wb = w_get(("win", 10 + half))
                for s in range(ns):
                    pb = tm_block(Wt, Wb, s, hT, [b_hT[kc][s] for kc in range(16)])
                    act(rs[:, s, half * 512:(half + 1) * 512], pb.f, AF.Silu, reads=[pb.buf], writes=[b_rs[s]])

        sc_rot = Rot([bk[4], bk[5]])
        yt_rot = Rot([bk[6], bk[7]])

        def gla_a(s):
            pb = sc_rot.next()
            pv = pb.f.rearrange("p (h i) -> p h i", h=4)
            for h in range(4):
                mm_group(pv[:, h, :], [(kT[:, h, s * 128:(s + 1) * 128], qT[:, h, s * 128:(s + 1) * 128])],
                         reads=[b_kT[h], b_qT[h]], writes=[pb.buf], same_ok=True)
            par = s % 2
            S.op("dve", lambda e: e.tensor_tensor(out=sTb[:, par, :, :], in0=pv,
                                                  in1=tri[:, :].unsqueeze(1).to_broadcast([128, 4, 128]), op=ALU.mult),
                 reads=[pb.buf, b_tri], writes=[b_sT[par]])

        def gla_state(s):
            pds = pair_f(1)
            pds_bufs = [bk[2].buf, bk[3].buf]
            for h in range(4):
                mm_group(pds[:, h * 256:(h + 1) * 256], [(ktm[:, s, h * 128:(h + 1) * 128], vtm[:, s, h * 256:(h + 1) * 256])],
                         reads=[b_ktm, b_vtm[s]], writes=pds_bufs, same_ok=True)
            Sflat = Sst[:, :, :].rearrange("p h e -> p (h e)")
            S.op("dve", lambda e: e.tensor_tensor(out=Sflat, in0=pds, in1=Sflat, op=ALU.add),
                 reads=pds_bufs, writes=[b_S])
            S.op("dve", lambda e: e.tensor_tensor(out=Sst[:, :, :], in0=Sst[:, :, :],
                                                  in1=ebt[:, :, s:s + 1].to_broadcast([128, 4, 256]), op=ALU.mult),
                 reads=[b_ebt], writes=[b_S])
            act(Sbf[:, :, :], Sst[:, :, :], AF.Copy, reads=[b_S], writes=[b_Sbf])

        def gla_b(s):
            par = s % 2
            po = pair_f(0)
            po_bufs = [bk[0].buf, bk[1].buf]
            for h in range(4):
                mm_group(po[:, h * 256:(h + 1) * 256],
                         [(sTb[:, par, h, :], vtm[:, s, h * 256:(h + 1) * 256]),
                          (qT[:, h, s * 128:(s + 1) * 128], Sbf[:, h, :])],
                         reads=[b_sT[par], b_vtm[s], b_qT[h], b_Sbf], writes=po_bufs, same_ok=True)
            gla_state(s)
            for h in range(4):
                act(jk[:, 0:256], po[:, h * 256:(h + 1) * 256], AF.Square, reads=po_bufs, writes=[b_ssqo[par]],
                    accum=ssq_o[:, par, h:h + 1])
            act(rstd_o[:, par, :], ssq_o[:, par, :], AF.Ln, reads=[b_ssqo[par]], writes=[b_rstdo[par]], scale=1.0 / 256,
                bias=epsb[:])
            act(rstd_o[:, par, :], rstd_o[:, par, :], AF.Exp, reads=[b_rstdo[par]], writes=[b_rstdo[par]], scale=-0.5)
            for h in range(4):
                S.op("dve", lambda e, h=h: e.scalar_tensor_tensor(
                    out=ygt[:, par, h * 256:(h + 1) * 256], in0=po[:, h * 256:(h + 1) * 256], scalar=rstd_o[:, par, h:h + 1],
                    in1=rs[:, s, h * 256:(h + 1) * 256], op0=ALU.mult, op1=ALU.mult),
                    reads=po_bufs + [b_rstdo[par], b_rs[s]], writes=[b_ygt[par]])

        def gla_c(s):
            par = s % 2
            pb = yt_rot.next()
            pv = pb.b.rearrange("p (c t) -> p c t", c=8)
            for c in range(8):
                S.op("pe", lambda e, c=c: e.transpose(out=pv[:, c, :], in_=ygt[:, par, c * 128:(c + 1) * 128], identity=identb[:]),
                     reads=[b_ygt[par]], writes=[pb.buf], same_ok=True)
            for ec in range(2):
                S.op("dve", lambda e, ec=ec: e.tensor_scalar(
                    out=yT[:, 8 + ec:16:2, s * 128:(s + 1) * 128], in0=pv[:, ec:8:2, :], scalar1=gnw_fm[:, ec:ec + 1],
                    scalar2=None, op0=ALU.mult),
                    reads=[pb.buf, b_const], writes=[b_yT[8 + 2 * h + ec][s] for h in range(4)])

        if not full:
            for s in range(ns):
                gla_state(s)
            continue

        def conv_copy_block(key, dst, dbufs):
            def run():
                Wt, Wb = w_get(key)
                for i in range(4):
                    pb = fm_block(Wt, Wb, ns, i)
                    act(dst[:, i, 0:N], pb.f[:, 0:N], AF.Copy, reads=[pb.buf], writes=[dbufs[i]])
            return run

        def conv_x_block(hf):
            def run():
                Wt, Wb = w_get(("win", 4 + hf))
                for i in range(4):
                    c = hf * 4 + i
                    ub = i % 2
                    pb = fm_block(Wt, Wb, ns, i)
                    S.op("dve", lambda e, ub=ub, c=c: e.tensor_copy(out=ubuf[:, ub, 0:2], in_=uhalo[:, c, :]),
                         reads=[b_uhalo[c]], writes=[b_u[ub]])
                    S.op("dve", lambda e, ub=ub, i=i, pb=pb, N=N: e.tensor_tensor(out=ubuf[:, ub, 2:2 + N], in0=pb.f[:, 0:N],
                                                                          in1=ccs[:, i, 0:N], op=ALU.mult),
                         reads=[pb.buf, b_ccs[i]], writes=[b_u[ub]])
                    S.op("dve", lambda e, ub=ub, c=c, N=N, tmc=tmc: e.tensor_scalar(out=uhalo[:, c, :], in0=ubuf[:, ub, N:N + 2], scalar1=tmc,
                                                                     scalar2=None, op0=ALU.mult),
                         reads=[b_u[ub], b_const], writes=[b_uhalo[c]])
                    act(cacc[:, ub, 0:N], ubuf[:, ub, 2:2 + N], AF.Identity, reads=[b_u[ub], b_const], writes=[b_cacc[ub]],
                        scale=convw_fm[:, 16 + c:17 + c])
                    S.op("dve", lambda e, ub=ub, c=c, N=N: e.scalar_tensor_tensor(
                        out=cacc[:, ub, 0:N], in0=ubuf[:, ub, 1:1 + N], scalar=convw_fm[:, 8 + c:9 + c], in1=cacc[:, ub, 0:N],
                        op0=ALU.mult, op1=ALU.add), reads=[b_u[ub], b_const], writes=[b_cacc[ub]])
                    S.op("dve", lambda e, ub=ub, c=c, N=N: e.scalar_tensor_tensor(
                        out=cacc[:, ub, 0:N], in0=ubuf[:, ub, 0:N], scalar=convw_fm[:, c:c + 1], in1=cacc[:, ub, 0:N],
                        op0=ALU.mult, op1=ALU.add), reads=[b_u[ub], b_const], writes=[b_cacc[ub]])
                    S.op("pool", lambda e, ub=ub, c=c, i=i, N=N: e.tensor_tensor(out=yT[:, c, 0:N], in0=cacc[:, ub, 0:N],
                                                                           in1=cbs[:, i, 0:N], op=ALU.mult),
                         reads=[b_cacc[ub], b_cbs[i]], writes=[b_yT[c][s] for s in range(ns)])
            return run

        conv_steps = []
        for hf in range(2):
            conv_steps.append(conv_copy_block(("win", 0 + hf), cbs, b_cbs))
            conv_steps.append(conv_copy_block(("win", 2 + hf), ccs, b_ccs))
            conv_steps.append(conv_x_block(hf))
        gla_steps = []
        for s in range(ns):
            gla_steps.append(lambda s=s: gla_a(s))
            gla_steps.append(lambda s=s: gla_b(s))
            gla_steps.append(lambda s=s: gla_c(s))
        gi = 0
        for ci, cstep in enumerate(conv_steps):
            for _ in range(2):
                if gi < len(gla_steps):
                    gla_steps[gi]()
                    gi += 1
            cstep()
        while gi < len(gla_steps):
            gla_steps[gi]()
            gi += 1

        alias_fence(tmp_bufs, b_ymix)
        for cg in range(4):
            Wt, Wb = w_get(("wout", cg))
            for s in range(ns):
                pb = tm_block(Wt, Wb, s, yT, [b_yT[c][s] for c in range(16)])
                act(ymix[:, s, cg * 512:(cg + 1) * 512], pb.f, AF.Copy, reads=[pb.buf], writes=[b_ymix[s]])
        for s in range(ns):
            act(hn[:, s % 2, :], ymix[:, s, :], AF.Square, reads=[b_ymix[s]], writes=[b_hn[s % 2], b_ss], accum=ssb[:, s:s + 1])
        act(rstd[:, 0:ns], ssb[:, 0:ns], AF.Ln, reads=[b_ss], writes=[b_rstd], scale=1.0 / D, bias=epsb[:])
        act(rstd[:, 0:ns], rstd[:, 0:ns], AF.Exp, reads=[b_rstd], writes=[b_rstd], scale=-0.5)
        for s in range(ns):
            S.op("dve", lambda e, s=s: e.scalar_tensor_tensor(out=ymix[:, s, :], in0=ymix[:, s, :], scalar=rstd[:, s:s + 1],
                                                             in1=Gm[:, :], op0=ALU.mult, op1=ALU.mult),
                 reads=[b_rstd, b_const], writes=[b_ymix[s]])
            S.op("pool", lambda e, s=s: e.tensor_tensor(out=X[:, s, :], in0=X[:, s, :], in1=ymix[:, s, :], op=ALU.add),
                 reads=[b_ymix[s]], writes=[bX[s]])

        stage_in(None, ns, A_f, B_f, from_X=True)
        alias_fence(mix_bufs, ffn_bufs)
        for b in range(11):
            for part in range(2):
                blk = b + 11 * part
                Wt, Wb = w_get(("wup", blk))
                for i in range(4):
                    gc = 4 * b + i
                    ch = gc + 44 * part
                    fb = i % 2
                    w0 = fconvw_fm[:, ch:ch + 1]
                    w1 = fconvw_fm[:, 88 + ch:89 + ch]
                    w2 = fconvw_fm[:, 176 + ch:177 + ch]
                    pb = fm_block(Wt, Wb, ns, i)
                    act(facc[:, fb, 0:N], pb.f[:, 0:N], AF.Identity, reads=[pb.buf, b_const], writes=[b_facc[fb]], scale=w2)
                    S.op("dve", lambda e, fb=fb, pb=pb, w1=w1, N=N: e.scalar_tensor_tensor(
                        out=facc[:, fb, 1:N], in0=pb.f[:, 0:N - 1], scalar=w1, in1=facc[:, fb, 1:N], op0=ALU.mult, op1=ALU.add),
                        reads=[pb.buf, b_const], writes=[b_facc[fb]])
                    S.op("dve", lambda e, fb=fb, pb=pb, w0=w0, N=N: e.scalar_tensor_tensor(
                        out=facc[:, fb, 2:N], in0=pb.f[:, 0:N - 2], scalar=w0, in1=facc[:, fb, 2:N], op0=ALU.mult, op1=ALU.add),
                        reads=[pb.buf, b_const], writes=[b_facc[fb]])
                    S.op("dve", lambda e, fb=fb, ch=ch, w1=w1: e.scalar_tensor_tensor(
                        out=facc[:, fb, 0:1], in0=fhalo[:, ch, 1:2], scalar=w1, in1=facc[:, fb, 0:1], op0=ALU.mult, op1=ALU.add),
                        reads=[b_fhalo[ch], b_const], writes=[b_facc[fb]])
                    S.op("dve", lambda e, fb=fb, ch=ch, w0=w0: e.scalar_tensor_tensor(
                        out=facc[:, fb, 0:2], in0=fhalo[:, ch, 0:2], scalar=w0, in1=facc[:, fb, 0:2], op0=ALU.mult, op1=ALU.add),
                        reads=[b_fhalo[ch], b_const], writes=[b_facc[fb]])
                    act(fhalo[:, ch, :], pb.f[:, N - 2:N], AF.Identity, reads=[pb.buf, b_const], writes=[b_fhalo[ch]], scale=tmc)
                    if part == 0:
                        act(sg[:, i, 0:N], facc[:, fb, 0:N], AF.Silu, reads=[b_facc[fb]], writes=[b_sg[i]])
                    elif kind == "main":
                        S.op("pool", lambda e, fb=fb, gc=gc, i=i, N=N: e.tensor_tensor(out=hid[:, gc, 0:N], in0=facc[:, fb, 0:N],
                                                                                 in1=sg[:, i, 0:N], op=ALU.mult),
                             reads=[b_facc[fb], b_sg[i]], writes=[b_hid[gc]])
        if kind != "main":
            if dbg:
                d_u = nc.dram_tensor("dbg_uhalo", [128, 16], F32, kind="ExternalOutput").ap()
                d_f = nc.dram_tensor("dbg_fhalo", [128, 176], F32, kind="ExternalOutput").ap()
                d_x = nc.dram_tensor("dbg_x1", [128, D], F32, kind="ExternalOutput").ap()
                sem_dbg = [S.new_dma_sem(f"semdbg{i}") for i in range(3)]
                dbg_toks = [
                    S.dma("sp", sem_dbg[0], lambda e: e.dma_start(out=d_u, in_=uhalo[:].rearrange("p a b -> p (a b)")), reads=b_uhalo),
                    S.dma("sp", sem_dbg[1], lambda e: e.dma_start(out=d_f, in_=fhalo[:].rearrange("p a b -> p (a b)")), reads=b_fhalo),
                    S.dma("sp", sem_dbg[2], lambda e: e.dma_start(out=d_x, in_=X[:, 0, :]), reads=[bX[0]])]
                S.final_wait("sp", dbg_toks)
            continue
        alias_fence(b_sg + b_facc, b_yffn)
        for cg in range(4):
            bset = [bk[(cg % 2) * 4 + s] for s in range(4)]
            for kb in range(3):
                nk = 16 if kb < 2 else 12
                Wt, Wb = w_get(("wdn", cg * 3 + kb))
                for s in range(ns):
                    pb = bset[s]

                    def fn(e, pb=pb, s=s, kb=kb, nk=nk, Wt=Wt):
                        r = None
                        for kc in range(nk):
                            r = e.matmul(pb.f, lhsT=hid[:, kb * 16 + kc, s * 128:(s + 1) * 128], rhs=Wt[:, kc, :],
                                         start=(kb == 0 and kc == 0), stop=(kb == 2 and kc == nk - 1))
                        return r
                    S.op("pe", fn, reads=[Wb] + b_hid[kb * 16:kb * 16 + nk], writes=[pb.buf], same_ok=True)
            for s in range(ns):
                act(yffn[:, s, cg * 512:(cg + 1) * 512], bset[s].f, AF.Copy, reads=[bset[s].buf], writes=[b_yffn[s]])
        for s in range(ns):
            act(hn[:, s % 2, :], yffn[:, s, :], AF.Square, reads=[b_yffn[s]], writes=[b_hn[s % 2], b_ss], accum=ssb[:, s:s + 1])
        act(rstd[:, 0:ns], ssb[:, 0:ns], AF.Ln, reads=[b_ss], writes=[b_rstd], scale=1.0 / D, bias=epsb[:])
        act(rstd[:, 0:ns], rstd[:, 0:ns], AF.Exp, reads=[b_rstd], writes=[b_rstd], scale=-0.5)
        for s in range(ns):
            S.op("dve", lambda e, s=s: e.scalar_tensor_tensor(out=yffn[:, s, :], in0=yffn[:, s, :], scalar=rstd[:, s:s + 1],
                                                             in1=Gf[:, :], op0=ALU.mult, op1=ALU.mult),
                 reads=[b_rstd, b_const], writes=[b_yffn[s]])
            S.op("pool", lambda e, s=s: e.tensor_tensor(out=X[:, s, :], in0=X[:, s, :], in1=yffn[:, s, :], op=ALU.add),
                 reads=[b_yffn[s]], writes=[bX[s]])
            S.dma("sp", sem_o[s], lambda e, s=s, odst=odst: e.dma_start(out=odst[s * 128:(s + 1) * 128, :], in_=X[:, s, :]),
                  reads=[bX[s]], writes=[b_out[s]])

    assert wst["next"] == len(plan), (wst["next"], len(plan))
    S.final_wait("sp", [b.w for b in b_out if b.w is not None])
    S.emit()
    return nc, S


def make_in_maps(x, c, w_mod, b_mod, mix_pre_w, mix_post_w, w_in, conv_w, gate_w2, gate_b, gla_norm_w, w_out,
                 ffn_pre_w, ffn_post_w, w_up, ffn_conv_w, w_down, NMAIN, NPRE, n_cores=8):
    B, SEQ, _ = x.shape
    per = n_cores // B
    seg = NMAIN * 512
    assert per * seg == SEQ
    NT = NPRE + 1 + NMAIN
    f = lambda a: np.ascontiguousarray(np.asarray(a, dtype=np.float32))
    shared = {
        "w_mod": f(w_mod[0]), "b_mod": f(b_mod[0]).reshape(96, 128), "pre_m": f(mix_pre_w[0]).reshape(16, 128),
        "post_m": f(mix_post_w[0]).reshape(16, 128), "w_in": f(w_in[0]), "conv_w": f(conv_w[0]).reshape(24, 128),
        "gate_w2": f(gate_w2[0]), "gate_b": f(gate_b[0]).reshape(1, 512), "gnw": f(gla_norm_w[0]).reshape(2, 128),
        "w_out": f(w_out[0]), "pre_f": f(ffn_pre_w[0]).reshape(16, 128), "post_f": f(ffn_post_w[0]).reshape(16, 128),
        "w_up": f(w_up[0]), "fconv_w": f(ffn_conv_w[0]).reshape(264, 128), "w_down": f(w_down[0]),
    }
    in_maps = []
    for core in range(n_cores):
        b, j = core // per, core % per
        t0 = j * seg
        xm = f(x[b, t0:t0 + seg])
        xp = np.zeros((NPRE * 512, D), np.float32)
        lo = t0 - NPRE * 512
        if lo >= 0:
            xp[:] = x[b, lo:t0]
        elif t0 > 0:
            xp[-t0:] = x[b, 0:t0]
        tmask = np.zeros((128, NT), np.float32)
        for i in range(NPRE):
            if lo + i * 512 >= 0:
                tmask[:, i] = 1.0
        tmask[:, NPRE] = 1.0 if t0 > 0 else 0.0
        tmask[:, NPRE + 1:] = 1.0
        m = dict(shared)
        m.update({"xmain": xm, "xpre": xp, "tmask": tmask, "cvec": f(c[b]).reshape(16, 128)})
        in_maps.append(m)
    return in_maps


_CACHE = {}


def kernel(x, c, w_mod, b_mod, mix_pre_w, mix_post_w, w_in, conv_w, gate_w2, gate_b, gla_norm_w, w_out,
           ffn_pre_w, ffn_post_w, w_up, ffn_conv_w, w_down):
    x = np.asarray(x)
    B, SEQ, _ = x.shape
    n_cores = 8
    per = n_cores // B
    NMAIN = SEQ // per // 512
    NPRE = (per - 1) * NMAIN
    key = (NMAIN, NPRE)
    if key not in _CACHE:
        _CACHE[key] = build_program(NMAIN, NPRE)[0]
    nc = _CACHE[key]
    in_maps = make_in_maps(x, c, w_mod, b_mod, mix_pre_w, mix_post_w, w_in, conv_w, gate_w2, gate_b, gla_norm_w, w_out,
                           ffn_pre_w, ffn_post_w, w_up, ffn_conv_w, w_down, NMAIN, NPRE, n_cores)
    res = run_bass_kernel_spmd(nc, in_maps, core_ids=list(range(n_cores)))
    out = np.empty((B, SEQ, D), np.float32)
    seg = NMAIN * 512
    for core in range(n_cores):
        b, j = core // per, core % per
        out[b, j * seg:(j + 1) * seg] = res.results[core]["out"]
    return out
```

```python
import numpy as np
import concourse.bass as bass
import concourse.mybir as mybir
from concourse.bass_utils import run_bass_kernel_spmd

F32 = mybir.dt.float32
BF16 = mybir.dt.bfloat16
AF = mybir.ActivationFunctionType
ALU = mybir.AluOpType

D = 2048
KC = 16
DIN = 6160
DFF = 5632
NUP = 11264
EPS = 1e-6


class Tok:
    __slots__ = ("sem", "val", "vc")

    def __init__(self, sem, val, vc):
        self.sem = sem
        self.val = val
        self.vc = vc


class Buf:
    __slots__ = ("name", "w", "r")

    def __init__(self, name):
        self.name = name
        self.w = None
        self.r = []


class Eng:
    def __init__(self, name, key):
        self.name = name
        self.key = key
        self.cnt = 0
        self.vc = {}
        self.ops = []


class Sched:
    ENGS = ("pe", "act", "dve", "pool", "sp")

    def __init__(self, nc):
        self.nc = nc
        self.sems = []
        self.engs = {}
        for n in self.ENGS:
            k = self._new_sem("prog_" + n)
            self.engs[n] = Eng(n, k)
        self.dma_sem_val = {}
        self.n_wait = 0
        self.n_ins = 0

    def _new_sem(self, name):
        h = self.nc.alloc_semaphore(name)
        self.sems.append(h)
        return len(self.sems) - 1

    def new_dma_sem(self, name):
        k = self._new_sem(name)
        self.dma_sem_val[k] = 0
        return k

    def _collect(self, reads, writes, extra):
        deps = []
        for b in reads:
            if b.w is not None:
                deps.append(b.w)
        for b in writes:
            if b.w is not None:
                deps.append(b.w)
            deps.extend(b.r)
        if extra:
            deps.extend(extra)
        return deps

    def _waits(self, eng, deps):
        best = {}
        evc = eng.vc
        for t in deps:
            if evc.get(t.sem, 0) >= t.val:
                continue
            b = best.get(t.sem)
            if b is None or b.val < t.val:
                best[t.sem] = t
        if not best:
            return []
        cand = list(best.values())
        chosen = []
        for t in cand:
            implied = False
            for t2 in cand:
                if t2 is not t and t2.vc.get(t.sem, 0) >= t.val:
                    implied = True
                    break
            if not implied:
                chosen.append((t.sem, t.val))
        for t in cand:
            if evc.get(t.sem, 0) < t.val:
                evc[t.sem] = t.val
            for k, v in t.vc.items():
                if evc.get(k, 0) < v:
                    evc[k] = v
        return chosen

    def op(self, engname, fn, reads=(), writes=(), extra=None, same_ok=False):
        eng = self.engs[engname]
        deps = self._collect(reads, writes, extra)
        if same_ok:
            deps = [t for t in deps if t.sem != eng.key]
        waits = self._waits(eng, deps)
        self.n_wait += len(waits)
        self.n_ins += 1
        eng.cnt += 1
        val = eng.cnt
        tok = Tok(eng.key, val, dict(eng.vc))
        sems = self.sems
        key = eng.key

        def thunk(e):
            for k, v in waits:
                e.wait_ge(sems[k], v)
            ins = fn(e)
            ins.then_inc(sems[key], 1)
        eng.ops.append(thunk)
        for b in reads:
            b.r.append(tok)
        for b in writes:
            b.w = tok
            b.r = []
        return tok

    def dma(self, engname, semkey, fn, reads=(), writes=(), extra=None):
        eng = self.engs[engname]
        deps = self._collect(reads, writes, extra)
        waits = self._waits(eng, deps)
        self.n_wait += len(waits)
        self.n_ins += 1
        self.dma_sem_val[semkey] += 16
        val = self.dma_sem_val[semkey]
        tok = Tok(semkey, val, dict(eng.vc))
        sems = self.sems

        def thunk(e):
            for k, v in waits:
                e.wait_ge(sems[k], v)
            fn(e).then_inc(sems[semkey], 16)
        eng.ops.append(thunk)
        for b in reads:
            b.r.append(tok)
        for b in writes:
            b.w = tok
            b.r = []
        return tok

    def final_wait(self, engname, toks):
        eng = self.engs[engname]
        waits = self._waits(eng, toks)
        sems = self.sems

        def thunk(e):
            for k, v in waits:
                e.wait_ge(sems[k], v)
        eng.ops.append(thunk)

    def emit(self):
        nc = self.nc
        engs = self.engs
        with nc.Block() as block:
            @block.tensor
            def _(e):
                for t in engs["pe"].ops:
                    t(e)

            @block.scalar
            def _(e):
                for t in engs["act"].ops:
                    t(e)

            @block.vector
            def _(e):
                for t in engs["dve"].ops:
                    t(e)

            @block.gpsimd
            def _(e):
                for t in engs["pool"].ops:
                    t(e)

            @block.sync
            def _(e):
                for t in engs["sp"].ops:
                    t(e)


def alias_fence(from_bufs, to_bufs):
    toks = []
    for b in from_bufs:
        if b.w is not None:
            toks.append(b.w)
        toks.extend(b.r)
    best = {}
    for t in toks:
        o = best.get(t.sem)
        if o is None or o.val < t.val:
            best[t.sem] = t
    toks = list(best.values())
    for b in to_bufs:
        b.r.extend(toks)


class Rot:
    def __init__(self, items):
        self.items = list(items)
        self.i = 0

    def next(self):
        r = self.items[self.i % len(self.items)]
        self.i += 1
        return r


def build_program(NMAIN, NPRE, dbg=False):
    NT = NPRE + 1 + NMAIN
    nc = bass.Bass("TRN2", target_bir_lowering=False)
    S = Sched(nc)

    def din(name, shape):
        return nc.dram_tensor(name, list(shape), F32, kind="ExternalInput").ap()

    xmain = din("xmain", [NMAIN * 512, D])
    xpre = din("xpre", [NPRE * 512, D])
    tmask_d = din("tmask", [128, NT])
    cvec = din("cvec", [16, 128])
    w_mod = din("w_mod", [D, 6 * D])
    b_mod = din("b_mod", [96, 128])
    pre_m = din("pre_m", [16, 128])
    post_m = din("post_m", [16, 128])
    w_in = din("w_in", [D, DIN])
    conv_w = din("conv_w", [24, 128])
    gate_w2 = din("gate_w2", [16, 512])
    gate_b = din("gate_b", [1, 512])
    gnw = din("gnw", [2, 128])
    w_out = din("w_out", [D, D])
    pre_f = din("pre_f", [16, 128])
    post_f = din("post_f", [16, 128])
    w_up = din("w_up", [D, NUP])
    fconv_w = din("fconv_w", [264, 128])
    w_down = din("w_down", [DFF, D])
    out_d = nc.dram_tensor("out", [NMAIN * 512, D], F32, kind="ExternalOutput").ap()

    s_win = nc.dram_tensor("s_win", [12, 128, 16, 512], BF16, kind="Internal").ap()
    s_wout = nc.dram_tensor("s_wout", [4, 128, 16, 512], BF16, kind="Internal").ap()
    s_wup = nc.dram_tensor("s_wup", [22, 128, 16, 512], BF16, kind="Internal").ap()
    s_wdn = nc.dram_tensor("s_wdn", [12, 128, 16, 512], BF16, kind="Internal").ap()

    sb_off = [((nc.sbuf_base + 63) // 64) * 64]
    sb_top = nc.sbuf_top

    def dsize(dt):
        return 4 if dt == F32 else 2

    def salloc(name, shape, dt, at=None):
        n = dsize(dt)
        for v in shape[1:]:
            n *= v
        if at is None:
            at = sb_off[0]
            sb_off[0] = ((at + n + 63) // 64) * 64
        assert at + n <= sb_top, (name, at, n, sb_top)
        return nc.alloc_sbuf_tensor_at(name, list(shape), dt, offset=at)

    X = salloc("X", [128, 4, D], F32)
    hT = salloc("hT", [128, 16, 512], BF16)
    Wsl = [salloc(f"W{i}", [128, 16, 512], BF16) for i in range(3)]
    AR = sb_off[0]
    ARENA = 79936
    sb_off[0] = AR + ARENA
    yT = salloc("yT", [128, 16, 512], BF16, at=AR + 0)
    vtm = salloc("vtm", [128, 4, 1024], BF16, at=AR + 16384)
    rs = salloc("rs", [128, 4, 1024], BF16, at=AR + 24576)
    qT = salloc("qT", [128, 4, 512], BF16, at=AR + 32768)
    kT = salloc("kT", [128, 4, 512], BF16, at=AR + 36864)
    ktm = salloc("ktm", [128, 4, 512], BF16, at=AR + 40960)
    cbs = salloc("cbs", [128, 4, 512], BF16, at=AR + 45056)
    ccs = salloc("ccs", [128, 4, 512], BF16, at=AR + 49152)
    sp = salloc("sp", [128, 2, 512], F32, at=AR + 53248)
    etmp = salloc("etmp", [128, 2, 512], F32, at=AR + 57344)
    Etmp = salloc("Etmp", [128, 2, 512], F32, at=AR + 61440)
    ubuf = salloc("ubuf", [128, 2, 514], F32, at=AR + 65536)
    cacc = salloc("cacc", [128, 2, 512], F32, at=AR + 69696)
    sTb = salloc("sTb", [128, 2, 4, 128], BF16, at=AR + 73792)
    ygt = salloc("ygt", [128, 2, 1024], BF16, at=AR + 75840)
    ymix = salloc("ymix", [128, 4, D], F32, at=AR + 16384)
    hid = salloc("hid", [128, 44, 512], BF16, at=AR + 0)
    sg = salloc("sg", [128, 4, 512], F32, at=AR + 45056)
    facc = salloc("facc", [128, 2, 512], F32, at=AR + 53248)
    yffn = salloc("yffn", [128, 4, D], F32, at=AR + 45056)
    vstage = salloc("vstage", [128, 128], F32, at=AR + 0)
    dtmp = salloc("dtmp", [128, 2, 128], F32, at=AR + 512)
    Gm = salloc("Gm", [128, D], BF16)
    Gf = salloc("Gf", [128, D], BF16)
    hn = salloc("hn", [128, 2, D], BF16)
    Sst = salloc("Sst", [128, 4, 256], F32)
    Sbf = salloc("Sbf", [128, 4, 256], BF16)
    gw2aug = salloc("gw2aug", [128, 512], F32)
    alrT = salloc("alrT", [128, 512], F32)
    identf = salloc("identf", [128, 128], F32)
    identb = salloc("identb", [128, 128], BF16)
    tri = salloc("tri", [128, 128], F32)
    ones = salloc("ones", [128, 128], F32)
    Walr = salloc("Walr", [128, 16, 16], BF16)
    bmod_fm = salloc("bmod_fm", [128, 96], F32)
    mod_fm = salloc("mod_fm", [128, 96], F32)
    prem_fm = salloc("prem_fm", [128, 16], F32)
    postm_fm = salloc("postm_fm", [128, 16], F32)
    pref_fm = salloc("pref_fm", [128, 16], F32)
    postf_fm = salloc("postf_fm", [128, 16], F32)
    A_m = salloc("A_m", [128, 16], F32)
    A_f = salloc("A_f", [128, 16], F32)
    Gm_fm = salloc("Gm_fm", [128, 16], F32)
    Gf_fm = salloc("Gf_fm", [128, 16], F32)
    convw_fm = salloc("convw_fm", [128, 24], F32)
    fconvw_fm = salloc("fconvw_fm", [128, 264], F32)
    gnw_fm = salloc("gnw_fm", [128, 2], F32)
    c_fm = salloc("c_fm", [128, 16], F32)
    ca_bf = salloc("ca_bf", [128, 16], BF16)
    tm = salloc("tm", [128, NT], F32)
    ssb = salloc("ssb", [128, 4], F32)
    rstd = salloc("rstd", [128, 4], F32)
    ssq_o = salloc("ssq_o", [128, 2, 4], F32)
    rstd_o = salloc("rstd_o", [128, 2, 4], F32)
    ebt = salloc("ebt", [128, 4, 4], F32)
    uhalo = salloc("uhalo", [128, 8, 2], F32)
    fhalo = salloc("fhalo", [128, 88, 2], F32)
    epsb = salloc("epsb", [128, 1], F32)
    jk = salloc("jk", [128, 512], BF16)

    pp = [nc.alloc_psum_tensor(f"pp{j}", [128, 1024], F32) for j in range(4)]

    class Bank:
        def __init__(self, i):
            self.i = i
            self.f = pp[i // 2][:, (i % 2) * 512:(i % 2 + 1) * 512]
            self.b = self.f.bitcast(BF16)
            self.buf = Buf(f"bank{i}")
    bk = [Bank(i) for i in range(8)]

    def pair_f(j):
        return pp[j][:, :]

    def pair_b(j):
        return pp[j][:, :].bitcast(BF16)

    bX = [Buf(f"X{s}") for s in range(4)]
    b_hT = [[Buf(f"hT{kc}_{s}") for s in range(4)] for kc in range(16)]
    b_hT_all = [b for row in b_hT for b in row]
    bW = [Buf(f"Wslot{i}") for i in range(3)]
    b_hn = [Buf("hn0"), Buf("hn1")]
    b_ss = Buf("ss")
    b_rstd = Buf("rstd")
    b_const = Buf("const")
    b_alrT = Buf("alrT")
    b_sp = [Buf("sp0"), Buf("sp1")]
    b_etmp = [Buf("et0"), Buf("et1")]
    b_Etmp = [Buf("E0"), Buf("E1")]
    b_qT = [Buf(f"qT{h}") for h in range(4)]
    b_kT = [Buf(f"kT{h}") for h in range(4)]
    b_ktm = Buf("ktm")
    b_vtm = [Buf(f"vtm{s}") for s in range(4)]
    b_rs = [Buf(f"rs{s}") for s in range(4)]
    b_ebt = Buf("ebt")
    b_sT = [Buf("sT0"), Buf("sT1")]
    b_ygt = [Buf("ygt0"), Buf("ygt1")]
    b_ssqo = [Buf("ssqo0"), Buf("ssqo1")]
    b_rstdo = [Buf("rstdo0"), Buf("rstdo1")]
    b_S = Buf("S")
    b_Sbf = Buf("Sbf")
    b_cbs = [Buf(f"cbs{i}") for i in range(4)]
    b_ccs = [Buf(f"ccs{i}") for i in range(4)]
    b_u = [Buf("u0"), Buf("u1")]
    b_cacc = [Buf("cacc0"), Buf("cacc1")]
    b_uhalo = [Buf(f"uhalo{c}") for c in range(8)]
    b_fhalo = [Buf(f"fhalo{c}") for c in range(88)]
    b_yT = [[Buf(f"yT{c}_{s}") for s in range(4)] for c in range(16)]
    b_ymix = [Buf(f"ymix{s}") for s in range(4)]
    b_hid = [Buf(f"hid{c}") for c in range(44)]
    b_sg = [Buf(f"sg{i}") for i in range(4)]
    b_facc = [Buf("facc0"), Buf("facc1")]
    b_yffn = [Buf(f"yffn{s}") for s in range(4)]
    b_out = [Buf(f"out{s}") for s in range(4)]
    mix_bufs = (b_sp + b_etmp + b_Etmp + b_qT + b_kT + [b_ktm] + b_vtm + b_rs + b_sT + b_ygt + b_cbs + b_ccs
                + b_u + b_cacc + [b for row in b_yT for b in row] + b_ymix)
    ffn_bufs = b_hid + b_sg + b_facc + b_yffn
    tmp_bufs = (b_sp + b_etmp + b_Etmp + b_qT + b_kT + [b_ktm] + b_vtm + b_rs + b_sT + b_ygt + b_cbs + b_ccs
                + b_u + b_cacc)

    sem_W = [S.new_dma_sem(f"semW{i}") for i in range(3)]
    sem_Wp = [S.new_dma_sem(f"semWp{i}") for i in range(3)]
    sem_x = [S.new_dma_sem(f"semx{i}") for i in range(4)]
    sem_o = [S.new_dma_sem(f"semo{i}") for i in range(4)]
    sem_cv = [S.new_dma_sem(f"semcv{i}") for i in range(4)]
    sem_misc = S.new_dma_sem("semmisc")
    sem_misc2 = S.new_dma_sem("semmisc2")
    sem_m3 = S.new_dma_sem("semm3")
    sem_m4 = S.new_dma_sem("semm4")
    sem_m5 = S.new_dma_sem("semm5")

    def mm_group(out_ap, pairs, reads, writes, same_ok=False):
        def fn(e):
            r = None
            n = len(pairs)
            for i, (l, rh) in enumerate(pairs):
                r = e.matmul(out_ap, lhsT=l, rhs=rh, start=(i == 0), stop=(i == n - 1))
            return r
        return S.op("pe", fn, reads=reads, writes=writes, same_ok=same_ok)

    def act(out, in_, func, reads, writes, scale=1.0, bias=None, accum=None):
        kw = {}
        if bias is not None:
            kw["bias"] = bias
        if accum is not None:
            kw["accum_out"] = accum
        return S.op("act", lambda e: e.activation(out=out, in_=in_, func=func, scale=scale, **kw),
                    reads=reads, writes=writes)

    misc_bufs = Buf("vstage")
    S.op("pool", lambda e: e.memset(identf[:], 1.0), writes=[b_const])
    S.op("pool", lambda e: e.affine_select(out=identf[:], in_=identf[:], pattern=[[-1, 128]],
                                           compare_op=ALU.is_equal, fill=0.0, base=0, channel_multiplier=1),
         writes=[b_const])
    S.op("pool", lambda e: e.tensor_copy(out=identb[:], in_=identf[:]), reads=[b_const], writes=[Buf("identb")])
    b_tri = Buf("tri")
    S.op("pool", lambda e: e.memset(tri[:], 1.0), writes=[b_tri])
    S.op("pool", lambda e: e.affine_select(out=tri[:], in_=tri[:], pattern=[[1, 128]],
                                           compare_op=ALU.is_ge, fill=0.0, base=0, channel_multiplier=-1),
         writes=[b_tri])
    S.op("pool", lambda e: e.memset(ones[:], 1.0), writes=[b_const])
    S.op("pool", lambda e: e.memset(epsb[:], EPS), writes=[b_const])
    S.op("pool", lambda e: e.memset(Sst[:], 0.0), writes=[b_S])
    S.op("pool", lambda e: e.memset(Sbf[:], 0.0), writes=[b_Sbf])
    S.op("pool", lambda e: e.memset(uhalo[:], 0.0), writes=b_uhalo)
    S.op("pool", lambda e: e.memset(fhalo[:], 0.0), writes=b_fhalo)
    S.op("pool", lambda e: e.memset(gw2aug[:], 0.0), writes=[b_const])
    S.op("pool", lambda e: e.memset(alrT[:], 1.0), writes=[b_alrT])
    S.dma("sp", sem_m3, lambda e: e.dma_start(out=tm[:], in_=tmask_d), writes=[b_const])
    S.dma("sp", sem_m4, lambda e: e.dma_start(out=gw2aug[0:16, :], in_=gate_w2), writes=[b_const])
    S.dma("sp", sem_m5, lambda e: e.dma_start(out=gw2aug[16:17, :], in_=gate_b), writes=[b_const])
    S.dma("pool", sem_misc2, lambda e: e.dma_start(out=Walr[:], in_=w_in[:, 6144:6160].rearrange("(kc p) n -> p kc n", p=128)),
          writes=[b_const])

    b_vst = Buf("vstage")
    vrot = Rot([bk[6], bk[7]])

    def vec_fm(src, n, dst):
        S.dma("sp", sem_misc, lambda e: e.dma_start(out=vstage[0:n, :], in_=src), writes=[b_vst])
        pb = vrot.next()
        S.op("pe", lambda e: e.transpose(out=pb.f[:, 0:n], in_=vstage[0:n, :], identity=identf[0:n, 0:n]),
             reads=[b_vst, b_const], writes=[pb.buf])
        S.op("dve", lambda e: e.tensor_copy(out=dst, in_=pb.f[:, 0:n]), reads=[pb.buf], writes=[b_const])

    vec_fm(cvec, 16, c_fm[:, :])
    vec_fm(b_mod, 96, bmod_fm[:, :])
    vec_fm(pre_m, 16, prem_fm[:, :])
    vec_fm(post_m, 16, postm_fm[:, :])
    vec_fm(pre_f, 16, pref_fm[:, :])
    vec_fm(post_f, 16, postf_fm[:, :])
    vec_fm(conv_w, 24, convw_fm[:, :])
    vec_fm(gnw, 2, gnw_fm[:, :])
    for t3 in range(3):
        vec_fm(fconv_w[t3 * 88:(t3 + 1) * 88, :], 88, fconvw_fm[:, t3 * 88:(t3 + 1) * 88])
    act(ca_bf[:], c_fm[:], AF.Silu, reads=[b_const], writes=[b_const])

    pmod = bk[5]
    for blk in range(24):
        sl = blk % 3
        S.dma("pool", sem_Wp[sl], lambda e, sl=sl, blk=blk: e.dma_start(
            out=Wsl[sl][:], in_=w_mod[:, blk * 512:(blk + 1) * 512].rearrange("(kc p) n -> p kc n", p=128)),
            writes=[bW[sl]])
        for nch in range(4):
            col = blk * 4 + nch
            mm_group(pmod.f[:, col:col + 1],
                     [(Wsl[sl][:, kc, nch * 128:(nch + 1) * 128], ca_bf[:, kc:kc + 1]) for kc in range(16)],
                     reads=[bW[sl], b_const], writes=[pmod.buf], same_ok=True)
    S.op("dve", lambda e: e.tensor_tensor(out=mod_fm[:], in0=pmod.f[:, 0:96], in1=bmod_fm[:], op=ALU.add),
         reads=[pmod.buf, b_const], writes=[b_const])
    S.op("dve", lambda e: e.scalar_tensor_tensor(out=A_m[:], in0=mod_fm[:, 16:32], scalar=1.0, in1=prem_fm[:],
                                                 op0=ALU.add, op1=ALU.mult), reads=[b_const], writes=[b_const])
    S.op("dve", lambda e: e.scalar_tensor_tensor(out=A_f[:], in0=mod_fm[:, 64:80], scalar=1.0, in1=pref_fm[:],
                                                 op0=ALU.add, op1=ALU.mult), reads=[b_const], writes=[b_const])
    S.op("dve", lambda e: e.tensor_tensor(out=Gm_fm[:], in0=mod_fm[:, 32:48], in1=postm_fm[:], op=ALU.mult),
         reads=[b_const], writes=[b_const])
    S.op("dve", lambda e: e.tensor_tensor(out=Gf_fm[:], in0=mod_fm[:, 80:96], in1=postf_fm[:], op=ALU.mult),
         reads=[b_const], writes=[b_const])
    B_m = mod_fm[:, 0:16]
    B_f = mod_fm[:, 48:64]
    b_dtmp = [Buf("dtmp0"), Buf("dtmp1")]
    grot = Rot([bk[6], bk[7]])
    for (gfm, gdst) in ((Gm_fm, Gm), (Gf_fm, Gf)):
        for g4 in range(4):
            pb = grot.next()
            for j in range(4):
                c = g4 * 4 + j
                dd = c % 2
                S.op("dve", lambda e, c=c, dd=dd, gfm=gfm: e.tensor_scalar(
                    out=dtmp[:, dd, :], in0=identf[:], scalar1=gfm[:, c:c + 1], scalar2=None, op0=ALU.mult),
                    reads=[b_const], writes=[b_dtmp[dd]])
                mm_group(pb.f[:, j * 128:(j + 1) * 128], [(ones[:], dtmp[:, dd, :])],
                         reads=[b_const, b_dtmp[dd]], writes=[pb.buf], same_ok=True)
            S.op("dve", lambda e, pb=pb, g4=g4, gdst=gdst: e.tensor_copy(out=gdst[:, g4 * 512:(g4 + 1) * 512], in_=pb.f),
                 reads=[pb.buf], writes=[b_const])

    cv_toks = []
    scr_buf = {}

    def convert(key, dst, src):
        i = len(cv_toks)
        b = Buf("scr")
        extra = [cv_toks[i - 4]] if i >= 4 else None
        t = S.dma("pool", sem_cv[i % 4], lambda e: e.dma_start(out=dst, in_=src), writes=[b], extra=extra)
        cv_toks.append(t)
        scr_buf[key] = b

    def win_src(blk):
        return w_in[:, blk * 512:(blk + 1) * 512].rearrange("(kc p) n -> p kc n", p=128)

    for blk in (7, 8, 9, 6, 10, 11, 0, 2, 4, 1, 3, 5):
        convert(("win", blk), s_win[blk], win_src(blk))
    for blk in range(4):
        convert(("wout", blk), s_wout[blk], w_out[:, blk * 512:(blk + 1) * 512].rearrange("(kc p) n -> p kc n", p=128))
    for b in range(11):
        for blk in (b, 11 + b):
            convert(("wup", blk), s_wup[blk], w_up[:, blk * 512:(blk + 1) * 512].rearrange("(kc p) n -> p kc n", p=128))
    for cg in range(4):
        for kb in range(3):
            nk = 16 if kb < 2 else 12
            convert(("wdn", cg * 3 + kb), s_wdn[cg * 3 + kb][:, 0:nk, :],
                    w_down[kb * 2048:kb * 2048 + nk * 128, cg * 512:(cg + 1) * 512].rearrange("(kc p) n -> p kc n", p=128))

    tiles = []
    for i in range(NPRE):
        ns = 4 if i < NPRE - 1 else 3
        tiles.append(("pre", xpre[i * 512:i * 512 + ns * 128, :], ns, i, None))
    tiles.append(("warm", xpre[NPRE * 512 - 128:NPRE * 512, :], 1, NPRE, None))
    for i in range(NMAIN):
        tiles.append(("main", xmain[i * 512:(i + 1) * 512, :], 4, NPRE + 1 + i, out_d[i * 512:(i + 1) * 512, :]))

    plan = []
    for ti_, (kind, _, _, _, _) in enumerate(tiles):
        if kind == "pre":
            seq = [("win", 7), ("win", 8), ("win", 9)] if ti_ == 0 else []
        else:
            seq = [("win", b) for b in (7, 6, 8, 9, 10, 11, 0, 2, 4, 1, 3, 5)]
            seq += [("wout", b) for b in range(4)]
            for b in range(11):
                seq += [("wup", b), ("wup", 11 + b)]
            if kind == "main":
                seq += [("wdn", i) for i in range(12)]
        plan.extend(seq)
    scr_ap = {"win": s_win, "wout": s_wout, "wup": s_wup, "wdn": s_wdn}
    wst = {"issued": 0, "next": 0, "limit": 3 if NPRE > 0 else len(plan)}

    def w_issue_upto(n):
        while wst["issued"] < min(n, len(plan), wst["limit"]):
            i = wst["issued"]
            sl = i % 3
            key = plan[i]
            nk = 12 if (key[0] == "wdn" and key[1] % 3 == 2) else 16
            src = scr_ap[key[0]][key[1]][:, 0:nk, :]
            S.dma("sp", sem_W[sl], lambda e, sl=sl, nk=nk, src=src: e.dma_start(out=Wsl[sl][:, 0:nk, :], in_=src),
                  reads=[scr_buf[key]], writes=[bW[sl]])
            wst["issued"] += 1

    def w_get(expect, resident=None):
        if resident is not None:
            return Wsl[resident], bW[resident]
        i = wst["next"]
        assert plan[i] == expect, (i, plan[i], expect)
        w_issue_upto(i + 3)
        wst["next"] += 1
        return Wsl[i % 3], bW[i % 3]

    allbanks = Rot(bk)

    b_ssl = [Buf(f"ss{s}") for s in range(4)]
    b_rstdl = [Buf(f"rstd{s}") for s in range(4)]

    def stage_in(xsrc, ns, A, B, from_X):
        for s in range(ns):
            if not from_X:
                S.dma("sp", sem_x[s], lambda e, s=s: e.dma_start(out=X[:, s, :], in_=xsrc[s * 128:(s + 1) * 128, :]),
                      writes=[bX[s]])
        for s in range(ns):
            act(hn[:, (s + 1) % 2, :], X[:, s, :], AF.Square, reads=[bX[s]], writes=[b_hn[(s + 1) % 2], b_ssl[s]],
                accum=ssb[:, s:s + 1])
            act(rstd[:, s:s + 1], ssb[:, s:s + 1], AF.Ln, reads=[b_ssl[s]], writes=[b_rstdl[s]], scale=1.0 / D, bias=epsb[:])
            act(rstd[:, s:s + 1], rstd[:, s:s + 1], AF.Exp, reads=[b_rstdl[s]], writes=[b_rstdl[s]], scale=-0.5)
        trot = Rot([bk[0], bk[1], bk[2], bk[3]])
        for s in range(ns):
            hb = s % 2
            S.op("dve", lambda e, s=s, hb=hb: e.tensor_scalar(out=hn[:, hb, :], in0=X[:, s, :], scalar1=rstd[:, s:s + 1],
                                                             scalar2=None, op0=ALU.mult),
                 reads=[bX[s], b_rstdl[s]], writes=[b_hn[hb]])
            for g in range(4):
                pb = trot.next()
                pv = pb.b[:, 0:512].rearrange("p (a b) -> p a b", a=4)
                for j in range(4):
                    kc = g * 4 + j
                    S.op("pe", lambda e, kc=kc, j=j, hb=hb, pv=pv: e.transpose(
                        out=pv[:, j, :], in_=hn[:, hb, kc * 128:(kc + 1) * 128], identity=identb[:]),
                        reads=[b_hn[hb]], writes=[pb.buf], same_ok=True)
                for j in range(4):
                    kc = g * 4 + j
                    if g % 2 == 0:
                        S.op("dve", lambda e, kc=kc, j=j, s=s, pv=pv: e.tensor_scalar(
                            out=hT[:, kc, s * 128:(s + 1) * 128], in0=pv[:, j, :], scalar1=A[:, kc:kc + 1],
                            scalar2=B[:, kc:kc + 1], op0=ALU.mult, op1=ALU.add),
                            reads=[pb.buf, b_const], writes=[b_hT[kc][s]])
                    else:
                        act(hT[:, kc, s * 128:(s + 1) * 128], pv[:, j, :], AF.Identity, reads=[pb.buf, b_const],
                            writes=[b_hT[kc][s]], scale=A[:, kc:kc + 1], bias=B[:, kc:kc + 1])

    def hT_reads(ns):
        return [b_hT[kc][s] for kc in range(16) for s in range(ns)]

    def fm_block(Wt, Wb, ns, i):
        N = ns * 128
        pb = allbanks.next()
        mm_group(pb.f[:, 0:N], [(Wt[:, kc, i * 128:(i + 1) * 128], hT[:, kc, 0:N]) for kc in range(16)],
                 reads=[Wb] + hT_reads(ns), writes=[pb.buf])
        return pb

    def tm_block(Wt, Wb, s, src, src_bufs, nk=16, kofs=0):
        pb = allbanks.next()
        mm_group(pb.f, [(src[:, kofs + kc, s * 128:(s + 1) * 128], Wt[:, kc, :]) for kc in range(nk)],
                 reads=[Wb] + src_bufs, writes=[pb.buf])
        return pb

    def do_stage_A(tile):
        alias_fence(ffn_bufs, mix_bufs)
        if tile[3] == 0:
            alias_fence([b_vst] + b_dtmp, mix_bufs)
        stage_in(tile[1], tile[2], A_m, B_m, from_X=False)

    staged = set()
    for ti, (kind, xsrc, ns, mcol, odst) in enumerate(tiles):
        N = ns * 128
        full = kind != "pre"
        tmc = tm[:, mcol:mcol + 1]
        res_w = (kind == "pre" and ti > 0)
        if ti not in staged:
            do_stage_A(tiles[ti])

        pa = bk[4]
        mm_group(pa.f[0:16, 0:N], [(Walr[:, kc, :], hT[:, kc, 0:N]) for kc in range(16)],
                 reads=[b_const] + hT_reads(ns), writes=[pa.buf])
        act(alrT[0:16, 0:N], pa.f[0:16, 0:N], AF.Copy, reads=[pa.buf], writes=[b_alrT])
        pnb = [bk[0], bk[1], bk[2], bk[3]]
        zrot = Rot([bk[5], bk[6]])
        for s in range(ns):
            pz = zrot.next()
            mm_group(pz.f, [(alrT[0:32, s * 128:(s + 1) * 128], gw2aug[0:32, :])], reads=[b_alrT, b_const], writes=[pz.buf])
            sb_ = s % 2
            act(etmp[:, sb_, :], pz.f, AF.Exp, reads=[pz.buf], writes=[b_etmp[sb_]], scale=-1.0)
            act(sp[:, sb_, :], etmp[:, sb_, :], AF.Ln, reads=[b_etmp[sb_], b_const], writes=[b_sp[sb_]], scale=1.0, bias=ones[:, 0:1])
            for h in range(4):
                mm_group(pnb[h].f[:, s * 128:(s + 1) * 128], [(sp[:, sb_, h * 128:(h + 1) * 128], tri[:])],
                         reads=[b_sp[sb_], b_tri], writes=[pnb[h].buf], same_ok=True)
        for h in range(4):
            act(ebt[:, h, 0:ns], pnb[h].f[:, 127:N:128], AF.Exp, reads=[pnb[h].buf], writes=[b_ebt], scale=-1.0 / 16)

        Wt, Wb = w_get(("win", 7), resident=0 if res_w else None)
        kq_rot = Rot([bk[4], bk[5]])
        pktv = pair_b(3).rearrange("p (s h d) -> p s h d", s=4, h=4)
        pkt_bufs = [bk[6].buf, bk[7].buf]
        for h in range(4):
            pb = kq_rot.next()
            mm_group(pb.f[:, 0:N], [(Wt[:, kc, h * 128:(h + 1) * 128], hT[:, kc, 0:N]) for kc in range(16)],
                     reads=[Wb] + hT_reads(ns), writes=[pb.buf])
            eb_ = h % 2
            act(Etmp[:, eb_, 0:N], pnb[h].f[:, 0:N], AF.Exp, reads=[pnb[h].buf], writes=[b_Etmp[eb_]], scale=1.0 / 16)
            S.op("dve", lambda e, h=h, pb=pb, eb_=eb_, N=N: e.tensor_tensor(out=kT[:, h, 0:N], in0=pb.f[:, 0:N], in1=Etmp[:, eb_, 0:N],
                                                                    op=ALU.mult),
                 reads=[pb.buf, b_Etmp[eb_]], writes=[b_kT[h]])
            for s in range(ns):
                S.op("pe", lambda e, h=h, s=s: e.transpose(out=pktv[:, s, h, :], in_=kT[:, h, s * 128:(s + 1) * 128],
                                                          identity=identb[:]),
                     reads=[b_kT[h]], writes=pkt_bufs, same_ok=True)
        S.op("act", lambda e, ns=ns: e.activation(out=ktm[:, 0:ns, :], in_=pair_b(3)[:, 0:ns * 512].rearrange("p (s n) -> p s n", s=ns),
                                           func=AF.Copy),
             reads=pkt_bufs, writes=[b_ktm])

        if full:
            Wt, Wb = w_get(("win", 6))
            for h in range(4):
                pb = kq_rot.next()
                mm_group(pb.f[:, 0:N], [(Wt[:, kc, h * 128:(h + 1) * 128], hT[:, kc, 0:N]) for kc in range(16)],
                         reads=[Wb] + hT_reads(ns), writes=[pb.buf])
                eb_ = h % 2
                act(Etmp[:, eb_, 0:N], pnb[h].f[:, 0:N], AF.Exp, reads=[pnb[h].buf], writes=[b_Etmp[eb_]], scale=-1.0 / 16)
                S.op("dve", lambda e, h=h, pb=pb, eb_=eb_, N=N: e.scalar_tensor_tensor(
                    out=qT[:, h, 0:N], in0=pb.f[:, 0:N], scalar=128.0 ** -0.5, in1=Etmp[:, eb_, 0:N],
                    op0=ALU.mult, op1=ALU.mult),
                    reads=[pb.buf, b_Etmp[eb_]], writes=[b_qT[h]])

        for half in range(2):
            Wt, Wb = w_get(("win", 8 + half), resident=(1 + half) if res_w else None)
            for s in range(ns):
                pb = tm_block(Wt, Wb, s, hT, [b_hT[kc][s] for kc in range(16)])
                act(vtm[:, s, half * 512:(half + 1) * 512], pb.f, AF.Identity, reads=[pb.buf, b_const], writes=[b_vtm[s]],
                    scale=tmc)
        if full:
            for half in range(2):
                Wt, Wb = w_get(("win", 10 + half))
                for s in range(ns):
                    pb = tm_block(Wt, Wb, s, hT, [b_hT[kc][s] for kc in range(16)])
                    act(rs[:, s, half * 512:(half + 1) * 512], pb.f, AF.Silu, reads=[pb.buf], writes=[b_rs[s]])

        sc_rot = Rot([bk[4], bk[5]])
        yt_rot = Rot([bk[6], bk[7]])

        def gla_a(s):
            pb = sc_rot.next()
            pv = pb.f.rearrange("p (h i) -> p h i", h=4)
            for h in range(4):
                mm_group(pv[:, h, :], [(kT[:, h, s * 128:(s + 1) * 128], qT[:, h, s * 128:(s + 1) * 128])],
                         reads=[b_kT[h], b_qT[h]], writes=[pb.buf], same_ok=True)
            par = s % 2
            S.op("dve", lambda e: e.tensor_tensor(out=sTb[:, par, :, :], in0=pv,
                                                  in1=tri[:, :].unsqueeze(1).to_broadcast([128, 4, 128]), op=ALU.mult),
                 reads=[pb.buf, b_tri], writes=[b_sT[par]])

        def gla_state(s):
            pds = pair_f(1)
            pds_bufs = [bk[2].buf, bk[3].buf]
            for h in range(4):
                mm_group(pds[:, h * 256:(h + 1) * 256], [(ktm[:, s, h * 128:(h + 1) * 128], vtm[:, s, h * 256:(h + 1) * 256])],
                         reads=[b_ktm, b_vtm[s]], writes=pds_bufs, same_ok=True)
            Sflat = Sst[:, :, :].rearrange("p h e -> p (h e)")
            S.op("dve", lambda e: e.tensor_tensor(out=Sflat, in0=pds, in1=Sflat, op=ALU.add),
                 reads=pds_bufs, writes=[b_S])
            S.op("dve", lambda e: e.tensor_tensor(out=Sst[:, :, :], in0=Sst[:, :, :],
                                                  in1=ebt[:, :, s:s + 1].to_broadcast([128, 4, 256]), op=ALU.mult),
                 reads=[b_ebt], writes=[b_S])
            act(Sbf[:, :, :], Sst[:, :, :], AF.Copy, reads=[b_S], writes=[b_Sbf])

        def gla_b(s):
            par = s % 2
            po = pair_f(0)
            po_bufs = [bk[0].buf, bk[1].buf]
            for h in range(4):
                mm_group(po[:, h * 256:(h + 1) * 256],
                         [(sTb[:, par, h, :], vtm[:, s, h * 256:(h + 1) * 256]),
                          (qT[:, h, s * 128:(s + 1) * 128], Sbf[:, h, :])],
                         reads=[b_sT[par], b_vtm[s], b_qT[h], b_Sbf], writes=po_bufs, same_ok=True)
            gla_state(s)
            for h in range(4):
                act(jk[:, 0:256], po[:, h * 256:(h + 1) * 256], AF.Square, reads=po_bufs, writes=[b_ssqo[par]],
                    accum=ssq_o[:, par, h:h + 1])
            act(rstd_o[:, par, :], ssq_o[:, par, :], AF.Ln, reads=[b_ssqo[par]], writes=[b_rstdo[par]], scale=1.0 / 256,
                bias=epsb[:])
            act(rstd_o[:, par, :], rstd_o[:, par, :], AF.Exp, reads=[b_rstdo[par]], writes=[b_rstdo[par]], scale=-0.5)
            for h in range(4):
                S.op("dve", lambda e, h=h: e.scalar_tensor_tensor(
                    out=ygt[:, par, h * 256:(h + 1) * 256], in0=po[:, h * 256:(h + 1) * 256], scalar=rstd_o[:, par, h:h + 1],
                    in1=rs[:, s, h * 256:(h + 1) * 256], op0=ALU.mult, op1=ALU.mult),
                    reads=po_bufs + [b_rstdo[par], b_rs[s]], writes=[b_ygt[par]])

        def gla_c(s):
            par = s % 2
            pb = yt_rot.next()
            pv = pb.b.rearrange("p (c t) -> p c t", c=8)
            for c in range(8):
                S.op("pe", lambda e, c=c: e.transpose(out=pv[:, c, :], in_=ygt[:, par, c * 128:(c + 1) * 128], identity=identb[:]),
                     reads=[b_ygt[par]], writes=[pb.buf], same_ok=True)
            for ec in range(2):
                S.op("dve", lambda e, ec=ec: e.tensor_scalar(
                    out=yT[:, 8 + ec:16:2, s * 128:(s + 1) * 128], in0=pv[:, ec:8:2, :], scalar1=gnw_fm[:, ec:ec + 1],
                    scalar2=None, op0=ALU.mult),
                    reads=[pb.buf, b_const], writes=[b_yT[8 + 2 * h + ec][s] for h in range(4)])

        if not full:
            if ti + 1 < len(tiles) and tiles[ti + 1][0] != "pre":
                wst["limit"] = len(plan)
                w_issue_upto(wst["next"] + 2)
            if ti + 1 < len(tiles):
                do_stage_A(tiles[ti + 1])
                staged.add(ti + 1)
            for s in range(ns):
                gla_state(s)
            continue

        def conv_copy_block(key, dst, dbufs):
            def run():
                Wt, Wb = w_get(key)
                for i in range(4):
                    pb = fm_block(Wt, Wb, ns, i)
                    act(dst[:, i, 0:N], pb.f[:, 0:N], AF.Copy, reads=[pb.buf], writes=[dbufs[i]])
            return run

        def conv_x_block(hf):
            def run():
                Wt, Wb = w_get(("win", 4 + hf))
                for i in range(4):
                    c = hf * 4 + i
                    ub = i % 2
                    pb = fm_block(Wt, Wb, ns, i)
                    S.op("dve", lambda e, ub=ub, c=c: e.tensor_copy(out=ubuf[:, ub, 0:2], in_=uhalo[:, c, :]),
                         reads=[b_uhalo[c]], writes=[b_u[ub]])
                    S.op("dve", lambda e, ub=ub, i=i, pb=pb, N=N: e.tensor_tensor(out=ubuf[:, ub, 2:2 + N], in0=pb.f[:, 0:N],
                                                                          in1=ccs[:, i, 0:N], op=ALU.mult),
                         reads=[pb.buf, b_ccs[i]], writes=[b_u[ub]])
                    S.op("dve", lambda e, ub=ub, c=c, N=N, tmc=tmc: e.tensor_scalar(out=uhalo[:, c, :], in0=ubuf[:, ub, N:N + 2], scalar1=tmc,
                                                                     scalar2=None, op0=ALU.mult),
                         reads=[b_u[ub], b_const], writes=[b_uhalo[c]])
                    act(cacc[:, ub, 0:N], ubuf[:, ub, 2:2 + N], AF.Identity, reads=[b_u[ub], b_const], writes=[b_cacc[ub]],
                        scale=convw_fm[:, 16 + c:17 + c])
                    S.op("dve", lambda e, ub=ub, c=c, N=N: e.scalar_tensor_tensor(
                        out=cacc[:, ub, 0:N], in0=ubuf[:, ub, 1:1 + N], scalar=convw_fm[:, 8 + c:9 + c], in1=cacc[:, ub, 0:N],
                        op0=ALU.mult, op1=ALU.add), reads=[b_u[ub], b_const], writes=[b_cacc[ub]])
                    S.op("dve", lambda e, ub=ub, c=c, N=N: e.scalar_tensor_tensor(
                        out=cacc[:, ub, 0:N], in0=ubuf[:, ub, 0:N], scalar=convw_fm[:, c:c + 1], in1=cacc[:, ub, 0:N],
                        op0=ALU.mult, op1=ALU.add), reads=[b_u[ub], b_const], writes=[b_cacc[ub]])
                    S.op("pool", lambda e, ub=ub, c=c, i=i, N=N: e.tensor_tensor(out=yT[:, c, 0:N], in0=cacc[:, ub, 0:N],
                                                                           in1=cbs[:, i, 0:N], op=ALU.mult),
                         reads=[b_cacc[ub], b_cbs[i]], writes=[b_yT[c][s] for s in range(ns)])
            return run

        conv_steps = []
        for hf in range(2):
            conv_steps.append(conv_copy_block(("win", 0 + hf), cbs, b_cbs))
            conv_steps.append(conv_copy_block(("win", 2 + hf), ccs, b_ccs))
            conv_steps.append(conv_x_block(hf))
        gla_steps = [lambda: gla_a(0), lambda: gla_b(0)]
        for s in range(1, ns):
            gla_steps.append(lambda s=s: gla_a(s))
            gla_steps.append(lambda s=s: gla_c(s - 1))
            gla_steps.append(lambda s=s: gla_b(s))
        gla_steps.append(lambda: gla_c(ns - 1))
        gi = 0
        for ci, cstep in enumerate(conv_steps):
            for _ in range(2):
                if gi < len(gla_steps):
                    gla_steps[gi]()
                    gi += 1
            cstep()
        while gi < len(gla_steps):
            gla_steps[gi]()
            gi += 1

        alias_fence(tmp_bufs, b_ymix)
        for cg in range(4):
            Wt, Wb = w_get(("wout", cg))
            for s in range(ns):
                pb = tm_block(Wt, Wb, s, yT, [b_yT[c][s] for c in range(16)])
                act(ymix[:, s, cg * 512:(cg + 1) * 512], pb.f, AF.Copy, reads=[pb.buf], writes=[b_ymix[s]])
        for s in range(ns):
            act(hn[:, s % 2, :], ymix[:, s, :], AF.Square, reads=[b_ymix[s]], writes=[b_hn[s % 2], b_ssl[s]], accum=ssb[:, s:s + 1])
        act(rstd[:, 0:ns], ssb[:, 0:ns], AF.Ln, reads=b_ssl[0:ns], writes=b_rstdl[0:ns], scale=1.0 / D, bias=epsb[:])
        act(rstd[:, 0:ns], rstd[:, 0:ns], AF.Exp, reads=b_rstdl[0:ns], writes=b_rstdl[0:ns], scale=-0.5)
        for s in range(ns):
            S.op("dve", lambda e, s=s: e.scalar_tensor_tensor(out=ymix[:, s, :], in0=ymix[:, s, :], scalar=rstd[:, s:s + 1],
                                                             in1=Gm[:, :], op0=ALU.mult, op1=ALU.mult),
                 reads=[b_rstdl[s], b_const], writes=[b_ymix[s]])
            S.op("pool", lambda e, s=s: e.tensor_tensor(out=X[:, s, :], in0=X[:, s, :], in1=ymix[:, s, :], op=ALU.add),
                 reads=[b_ymix[s]], writes=[bX[s]])

        stage_in(None, ns, A_f, B_f, from_X=True)
        alias_fence(mix_bufs, ffn_bufs)
        for b in range(11):
            for part in range(2):
                blk = b + 11 * part
                Wt, Wb = w_get(("wup", blk))
                for i in range(4):
                    gc = 4 * b + i
                    ch = gc + 44 * part
                    fb = i % 2
                    w0 = fconvw_fm[:, ch:ch + 1]
                    w1 = fconvw_fm[:, 88 + ch:89 + ch]
                    w2 = fconvw_fm[:, 176 + ch:177 + ch]
                    pb = fm_block(Wt, Wb, ns, i)
                    act(facc[:, fb, 0:N], pb.f[:, 0:N], AF.Identity, reads=[pb.buf, b_const], writes=[b_facc[fb]], scale=w2)
                    S.op("dve", lambda e, fb=fb, pb=pb, w1=w1, N=N: e.scalar_tensor_tensor(
                        out=facc[:, fb, 1:N], in0=pb.f[:, 0:N - 1], scalar=w1, in1=facc[:, fb, 1:N], op0=ALU.mult, op1=ALU.add),
                        reads=[pb.buf, b_const], writes=[b_facc[fb]])
                    S.op("dve", lambda e, fb=fb, pb=pb, w0=w0, N=N: e.scalar_tensor_tensor(
                        out=facc[:, fb, 2:N], in0=pb.f[:, 0:N - 2], scalar=w0, in1=facc[:, fb, 2:N], op0=ALU.mult, op1=ALU.add),
                        reads=[pb.buf, b_const], writes=[b_facc[fb]])
                    S.op("dve", lambda e, fb=fb, ch=ch, w1=w1: e.scalar_tensor_tensor(
                        out=facc[:, fb, 0:1], in0=fhalo[:, ch, 1:2], scalar=w1, in1=facc[:, fb, 0:1], op0=ALU.mult, op1=ALU.add),
                        reads=[b_fhalo[ch], b_const], writes=[b_facc[fb]])
                    S.op("dve", lambda e, fb=fb, ch=ch, w0=w0: e.scalar_tensor_tensor(
                        out=facc[:, fb, 0:2], in0=fhalo[:, ch, 0:2], scalar=w0, in1=facc[:, fb, 0:2], op0=ALU.mult, op1=ALU.add),
                        reads=[b_fhalo[ch], b_const], writes=[b_facc[fb]])
                    act(fhalo[:, ch, :], pb.f[:, N - 2:N], AF.Identity, reads=[pb.buf, b_const], writes=[b_fhalo[ch]], scale=tmc)
                    if part == 0:
                        act(sg[:, i, 0:N], facc[:, fb, 0:N], AF.Silu, reads=[b_facc[fb]], writes=[b_sg[i]])
                    elif kind == "main":
                        S.op("pool", lambda e, fb=fb, gc=gc, i=i, N=N: e.tensor_tensor(out=hid[:, gc, 0:N], in0=facc[:, fb, 0:N],
                                                                                 in1=sg[:, i, 0:N], op=ALU.mult),
                             reads=[b_facc[fb], b_sg[i]], writes=[b_hid[gc]])
        if kind != "main":
            if dbg:
                d_u = nc.dram_tensor("dbg_uhalo", [128, 16], F32, kind="ExternalOutput").ap()
                d_f = nc.dram_tensor("dbg_fhalo", [128, 176], F32, kind="ExternalOutput").ap()
                d_x = nc.dram_tensor("dbg_x1", [128, D], F32, kind="ExternalOutput").ap()
                sem_dbg = [S.new_dma_sem(f"semdbg{i}") for i in range(3)]
                dbg_toks = [
                    S.dma("sp", sem_dbg[0], lambda e: e.dma_start(out=d_u, in_=uhalo[:].rearrange("p a b -> p (a b)")), reads=b_uhalo),
                    S.dma("sp", sem_dbg[1], lambda e: e.dma_start(out=d_f, in_=fhalo[:].rearrange("p a b -> p (a b)")), reads=b_fhalo),
                    S.dma("sp", sem_dbg[2], lambda e: e.dma_start(out=d_x, in_=X[:, 0, :]), reads=[bX[0]])]
                S.final_wait("sp", dbg_toks)
            continue
        alias_fence(b_sg + b_facc, b_yffn)
        for cg in range(4):
            bset = [bk[(cg % 2) * 4 + s] for s in range(4)]
            for kb in range(3):
                nk = 16 if kb < 2 else 12
                Wt, Wb = w_get(("wdn", cg * 3 + kb))
                for s in range(ns):
                    pb = bset[s]

                    def fn(e, pb=pb, s=s, kb=kb, nk=nk, Wt=Wt):
                        r = None
                        for kc in range(nk):
                            r = e.matmul(pb.f, lhsT=hid[:, kb * 16 + kc, s * 128:(s + 1) * 128], rhs=Wt[:, kc, :],
                                         start=(kb == 0 and kc == 0), stop=(kb == 2 and kc == nk - 1))
                        return r
                    S.op("pe", fn, reads=[Wb] + b_hid[kb * 16:kb * 16 + nk], writes=[pb.buf], same_ok=True)
            for s in range(ns):
                act(yffn[:, s, cg * 512:(cg + 1) * 512], bset[s].f, AF.Copy, reads=[bset[s].buf], writes=[b_yffn[s]])
        for s in range(ns):
            act(hn[:, s % 2, :], yffn[:, s, :], AF.Square, reads=[b_yffn[s]], writes=[b_hn[s % 2], b_ssl[s]], accum=ssb[:, s:s + 1])
        act(rstd[:, 0:ns], ssb[:, 0:ns], AF.Ln, reads=b_ssl[0:ns], writes=b_rstdl[0:ns], scale=1.0 / D, bias=epsb[:])
        act(rstd[:, 0:ns], rstd[:, 0:ns], AF.Exp, reads=b_rstdl[0:ns], writes=b_rstdl[0:ns], scale=-0.5)
        for s in range(ns):
            S.op("dve", lambda e, s=s: e.scalar_tensor_tensor(out=yffn[:, s, :], in0=yffn[:, s, :], scalar=rstd[:, s:s + 1],
                                                             in1=Gf[:, :], op0=ALU.mult, op1=ALU.mult),
                 reads=[b_rstdl[s], b_const], writes=[b_yffn[s]])
            S.op("pool", lambda e, s=s: e.tensor_tensor(out=yffn[:, s, :], in0=X[:, s, :], in1=yffn[:, s, :], op=ALU.add),
                 reads=[bX[s]], writes=[b_yffn[s]])
            S.dma("sp", sem_o[s], lambda e, s=s, odst=odst: e.dma_start(out=odst[s * 128:(s + 1) * 128, :], in_=yffn[:, s, :]),
                  reads=[b_yffn[s]], writes=[b_out[s]])

    assert wst["next"] == len(plan), (wst["next"], len(plan))
    S.final_wait("sp", [b.w for b in b_out if b.w is not None])
    S.emit()
    return nc, S


def make_in_maps(x, c, w_mod, b_mod, mix_pre_w, mix_post_w, w_in, conv_w, gate_w2, gate_b, gla_norm_w, w_out,
                 ffn_pre_w, ffn_post_w, w_up, ffn_conv_w, w_down, NMAIN, NPRE, n_cores=8):
    B, SEQ, _ = x.shape
    per = n_cores // B
    seg = NMAIN * 512
    assert per * seg == SEQ
    NT = NPRE + 1 + NMAIN
    f = lambda a: np.ascontiguousarray(np.asarray(a, dtype=np.float32))
    shared = {
        "w_mod": f(w_mod[0]), "b_mod": f(b_mod[0]).reshape(96, 128), "pre_m": f(mix_pre_w[0]).reshape(16, 128),
        "post_m": f(mix_post_w[0]).reshape(16, 128), "w_in": f(w_in[0]), "conv_w": f(conv_w[0]).reshape(24, 128),
        "gate_w2": f(gate_w2[0]), "gate_b": f(gate_b[0]).reshape(1, 512), "gnw": f(gla_norm_w[0]).reshape(2, 128),
        "w_out": f(w_out[0]), "pre_f": f(ffn_pre_w[0]).reshape(16, 128), "post_f": f(ffn_post_w[0]).reshape(16, 128),
        "w_up": f(w_up[0]), "fconv_w": f(ffn_conv_w[0]).reshape(264, 128), "w_down": f(w_down[0]),
    }
    in_maps = []
    for core in range(n_cores):
        b, j = core // per, core % per
        t0 = j * seg
        xm = f(x[b, t0:t0 + seg])
        xp = np.zeros((NPRE * 512, D), np.float32)
        lo = t0 - NPRE * 512
        if lo >= 0:
            xp[:] = x[b, lo:t0]
        elif t0 > 0:
            xp[-t0:] = x[b, 0:t0]
        tmask = np.zeros((128, NT), np.float32)
        for i in range(NPRE):
            if lo + i * 512 >= 0:
                tmask[:, i] = 1.0
        tmask[:, NPRE] = 1.0 if t0 > 0 else 0.0
        tmask[:, NPRE + 1:] = 1.0
        m = dict(shared)
        m.update({"xmain": xm, "xpre": xp, "tmask": tmask, "cvec": f(c[b]).reshape(16, 128)})
        in_maps.append(m)
    return in_maps


_CACHE = {}


def kernel(x, c, w_mod, b_mod, mix_pre_w, mix_post_w, w_in, conv_w, gate_w2, gate_b, gla_norm_w, w_out,
           ffn_pre_w, ffn_post_w, w_up, ffn_conv_w, w_down):
    x = np.asarray(x)
    B, SEQ, _ = x.shape
    n_cores = 8
    per = n_cores // B
    NMAIN = SEQ // per // 512
    NPRE = (per - 1) * NMAIN
    key = (NMAIN, NPRE)
    if key not in _CACHE:
        _CACHE[key] = build_program(NMAIN, NPRE)[0]
    nc = _CACHE[key]
    in_maps = make_in_maps(x, c, w_mod, b_mod, mix_pre_w, mix_post_w, w_in, conv_w, gate_w2, gate_b, gla_norm_w, w_out,
                           ffn_pre_w, ffn_post_w, w_up, ffn_conv_w, w_down, NMAIN, NPRE, n_cores)
    res = run_bass_kernel_spmd(nc, in_maps, core_ids=list(range(n_cores)))
    out = np.empty((B, SEQ, D), np.float32)
    seg = NMAIN * 512
    for core in range(n_cores):
        b, j = core // per, core % per
        out[b, j * seg:(j + 1) * seg] = res.results[core]["out"]
    return out
```

```python
import numpy as np
import concourse.bass as bass
import concourse.mybir as mybir
from concourse.bass_utils import run_bass_kernel_spmd

F32 = mybir.dt.float32
BF16 = mybir.dt.bfloat16
AF = mybir.ActivationFunctionType
ALU = mybir.AluOpType

D = 2048
KC = 16
DIN = 6160
DFF = 5632
NUP = 11264
EPS = 1e-6


class Tok:
    __slots__ = ("sem", "val", "vc")

    def __init__(self, sem, val, vc):
        self.sem = sem
        self.val = val
        self.vc = vc


class Buf:
    __slots__ = ("name", "w", "r")

    def __init__(self, name):
        self.name = name
        self.w = None
        self.r = []


class Eng:
    def __init__(self, name, key):
        self.name = name
        self.key = key
        self.cnt = 0
        self.vc = {}
        self.ops = []


class Sched:
    ENGS = ("pe", "act", "dve", "pool", "sp")

    def __init__(self, nc):
        self.nc = nc
        self.sems = []
        self.engs = {}
        for n in self.ENGS:
            k = self._new_sem("prog_" + n)
            self.engs[n] = Eng(n, k)
        self.dma_sem_val = {}
        self.n_wait = 0
        self.n_ins = 0

    def _new_sem(self, name):
        h = self.nc.alloc_semaphore(name)
        self.sems.append(h)
        return len(self.sems) - 1

    def new_dma_sem(self, name):
        k = self._new_sem(name)
        self.dma_sem_val[k] = 0
        return k

    def _collect(self, reads, writes, extra):
        deps = []
        for b in reads:
            if b.w is not None:
                deps.append(b.w)
        for b in writes:
            if b.w is not None:
                deps.append(b.w)
            deps.extend(b.r)
        if extra:
            deps.extend(extra)
        return deps

    def _waits(self, eng, deps):
        best = {}
        evc = eng.vc
        for t in deps:
            if evc.get(t.sem, 0) >= t.val:
                continue
            b = best.get(t.sem)
            if b is None or b.val < t.val:
                best[t.sem] = t
        if not best:
            return []
        cand = list(best.values())
        chosen = []
        for t in cand:
            implied = False
            for t2 in cand:
                if t2 is not t and t2.vc.get(t.sem, 0) >= t.val:
                    implied = True
                    break
            if not implied:
                chosen.append((t.sem, t.val))
        for t in cand:
            if evc.get(t.sem, 0) < t.val:
                evc[t.sem] = t.val
            for k, v in t.vc.items():
                if evc.get(k, 0) < v:
                    evc[k] = v
        return chosen

    def op(self, engname, fn, reads=(), writes=(), extra=None, same_ok=False):
        eng = self.engs[engname]
        deps = self._collect(reads, writes, extra)
        if same_ok:
            deps = [t for t in deps if t.sem != eng.key]
        waits = self._waits(eng, deps)
        self.n_wait += len(waits)
        self.n_ins += 1
        eng.cnt += 1
        val = eng.cnt
        tok = Tok(eng.key, val, dict(eng.vc))
        sems = self.sems
        key = eng.key

        def thunk(e):
            for k, v in waits:
                e.wait_ge(sems[k], v)
            ins = fn(e)
            ins.then_inc(sems[key], 1)
        eng.ops.append(thunk)
        for b in reads:
            b.r.append(tok)
        for b in writes:
            b.w = tok
            b.r = []
        return tok

    def dma(self, engname, semkey, fn, reads=(), writes=(), extra=None):
        eng = self.engs[engname]
        deps = self._collect(reads, writes, extra)
        waits = self._waits(eng, deps)
        self.n_wait += len(waits)
        self.n_ins += 1
        self.dma_sem_val[semkey] += 16
        val = self.dma_sem_val[semkey]
        tok = Tok(semkey, val, dict(eng.vc))
        sems = self.sems

        def thunk(e):
            for k, v in waits:
                e.wait_ge(sems[k], v)
            fn(e).then_inc(sems[semkey], 16)
        eng.ops.append(thunk)
        for b in reads:
            b.r.append(tok)
        for b in writes:
            b.w = tok
            b.r = []
        return tok

    def final_wait(self, engname, toks):
        eng = self.engs[engname]
        waits = self._waits(eng, toks)
        sems = self.sems

        def thunk(e):
            for k, v in waits:
                e.wait_ge(sems[k], v)
        eng.ops.append(thunk)

    def emit(self):
        nc = self.nc
        engs = self.engs
        with nc.Block() as block:
            @block.tensor
            def _(e):
                for t in engs["pe"].ops:
                    t(e)

            @block.scalar
            def _(e):
                for t in engs["act"].ops:
                    t(e)

            @block.vector
            def _(e):
                for t in engs["dve"].ops:
                    t(e)

            @block.gpsimd
            def _(e):
                for t in engs["pool"].ops:
                    t(e)

            @block.sync
            def _(e):
                for t in engs["sp"].ops:
                    t(e)


def alias_fence(from_bufs, to_bufs):
    toks = []
    for b in from_bufs:
        if b.w is not None:
            toks.append(b.w)
        toks.extend(b.r)
    best = {}
    for t in toks:
        o = best.get(t.sem)
        if o is None or o.val < t.val:
            best[t.sem] = t
    toks = list(best.values())
    for b in to_bufs:
        b.r.extend(toks)


class Rot:
    def __init__(self, items):
        self.items = list(items)
        self.i = 0

    def next(self):
        r = self.items[self.i % len(self.items)]
        self.i += 1
        return r


def build_program(NMAIN, NPRE, dbg=False):
    NT = NPRE + 1 + NMAIN
    nc = bass.Bass("TRN2", target_bir_lowering=False)
    S = Sched(nc)

    def din(name, shape):
        return nc.dram_tensor(name, list(shape), F32, kind="ExternalInput").ap()

    xmain = din("xmain", [NMAIN * 512, D])
    xpre = din("xpre", [NPRE * 512, D])
    tmask_d = din("tmask", [128, NT])
    cvec = din("cvec", [16, 128])
    w_mod = din("w_mod", [D, 6 * D])
    b_mod = din("b_mod", [96, 128])
    pre_m = din("pre_m", [16, 128])
    post_m = din("post_m", [16, 128])
    w_in = din("w_in", [D, DIN])
    conv_w = din("conv_w", [24, 128])
    gate_w2 = din("gate_w2", [16, 512])
    gate_b = din("gate_b", [1, 512])
    gnw = din("gnw", [2, 128])
    w_out = din("w_out", [D, D])
    pre_f = din("pre_f", [16, 128])
    post_f = din("post_f", [16, 128])
    w_up = din("w_up", [D, NUP])
    fconv_w = din("fconv_w", [264, 128])
    w_down = din("w_down", [DFF, D])
    out_d = nc.dram_tensor("out", [NMAIN * 512, D], F32, kind="ExternalOutput").ap()

    s_win = nc.dram_tensor("s_win", [12, 128, 16, 512], BF16, kind="Internal").ap()
    s_wout = nc.dram_tensor("s_wout", [4, 128, 16, 512], BF16, kind="Internal").ap()
    s_wup = nc.dram_tensor("s_wup", [22, 128, 16, 512], BF16, kind="Internal").ap()
    s_wdn = nc.dram_tensor("s_wdn", [12, 128, 16, 512], BF16, kind="Internal").ap()

    sb_off = [((nc.sbuf_base + 63) // 64) * 64]
    sb_top = nc.sbuf_top

    def dsize(dt):
        return 4 if dt == F32 else 2

    def salloc(name, shape, dt, at=None):
        n = dsize(dt)
        for v in shape[1:]:
            n *= v
        if at is None:
            at = sb_off[0]
            sb_off[0] = ((at + n + 63) // 64) * 64
        assert at + n <= sb_top, (name, at, n, sb_top)
        return nc.alloc_sbuf_tensor_at(name, list(shape), dt, offset=at)

    X = salloc("X", [128, 4, D], F32)
    hT = salloc("hT", [128, 16, 512], BF16)
    Wsl = [salloc(f"W{i}", [128, 16, 512], BF16) for i in range(3)]
    AR = sb_off[0]
    ARENA = 79936
    sb_off[0] = AR + ARENA
    yT = salloc("yT", [128, 16, 512], BF16, at=AR + 0)
    vtm = salloc("vtm", [128, 4, 1024], BF16, at=AR + 16384)
    rs = salloc("rs", [128, 4, 1024], BF16, at=AR + 24576)
    qT = salloc("qT", [128, 4, 512], BF16, at=AR + 32768)
    kT = salloc("kT", [128, 4, 512], BF16, at=AR + 36864)
    ktm = salloc("ktm", [128, 4, 512], BF16, at=AR + 40960)
    cbs = salloc("cbs", [128, 4, 512], BF16, at=AR + 45056)
    ccs = salloc("ccs", [128, 4, 512], BF16, at=AR + 49152)
    sp = salloc("sp", [128, 2, 512], F32, at=AR + 53248)
    etmp = salloc("etmp", [128, 2, 512], F32, at=AR + 57344)
    Etmp = salloc("Etmp", [128, 2, 512], F32, at=AR + 61440)
    ubuf = salloc("ubuf", [128, 2, 514], F32, at=AR + 65536)
    cacc = salloc("cacc", [128, 2, 512], F32, at=AR + 69696)
    sTb = salloc("sTb", [128, 2, 4, 128], BF16, at=AR + 73792)
    ygt = salloc("ygt", [128, 2, 1024], BF16, at=AR + 75840)
    ymix = salloc("ymix", [128, 4, D], F32, at=AR + 16384)
    hid = salloc("hid", [128, 44, 512], BF16, at=AR + 0)
    sg = salloc("sg", [128, 4, 512], F32, at=AR + 45056)
    facc = salloc("facc", [128, 2, 512], F32, at=AR + 53248)
    yffn = salloc("yffn", [128, 4, D], F32, at=AR + 45056)
    vstage = salloc("vstage", [128, 128], F32, at=AR + 0)
    dtmp = salloc("dtmp", [128, 2, 128], F32, at=AR + 512)
    Gm = salloc("Gm", [128, D], BF16)
    Gf = salloc("Gf", [128, D], BF16)
    hn = salloc("hn", [128, 2, D], BF16)
    Sst = salloc("Sst", [128, 4, 256], F32)
    Sbf = salloc("Sbf", [128, 4, 256], BF16)
    gw2aug = salloc("gw2aug", [128, 512], F32)
    alrT = salloc("alrT", [128, 512], F32)
    identf = salloc("identf", [128, 128], F32)
    identb = salloc("identb", [128, 128], BF16)
    tri = salloc("tri", [128, 128], F32)
    ones = salloc("ones", [128, 128], F32)
    Walr = salloc("Walr", [128, 16, 16], BF16)
    bmod_fm = salloc("bmod_fm", [128, 96], F32)
    mod_fm = salloc("mod_fm", [128, 96], F32)
    prem_fm = salloc("prem_fm", [128, 16], F32)
    postm_fm = salloc("postm_fm", [128, 16], F32)
    pref_fm = salloc("pref_fm", [128, 16], F32)
    postf_fm = salloc("postf_fm", [128, 16], F32)
    A_m = salloc("A_m", [128, 16], F32)
    A_f = salloc("A_f", [128, 16], F32)
    Gm_fm = salloc("Gm_fm", [128, 16], F32)
    Gf_fm = salloc("Gf_fm", [128, 16], F32)
    convw_fm = salloc("convw_fm", [128, 24], F32)
    fconvw_fm = salloc("fconvw_fm", [128, 264], F32)
    gnw_fm = salloc("gnw_fm", [128, 2], F32)
    c_fm = salloc("c_fm", [128, 16], F32)
    ca_bf = salloc("ca_bf", [128, 16], BF16)
    tm = salloc("tm", [128, NT], F32)
    ssb = salloc("ssb", [128, 4], F32)
    rstd = salloc("rstd", [128, 4], F32)
    ssq_o = salloc("ssq_o", [128, 2, 4], F32)
    rstd_o = salloc("rstd_o", [128, 2, 4], F32)
    ebt = salloc("ebt", [128, 4, 4], F32)
    uhalo = salloc("uhalo", [128, 8, 2], F32)
    fhalo = salloc("fhalo", [128, 88, 2], F32)
    epsb = salloc("epsb", [128, 1], F32)
    jk = salloc("jk", [128, 512], BF16)

    pp = [nc.alloc_psum_tensor(f"pp{j}", [128, 1024], F32) for j in range(4)]

    class Bank:
        def __init__(self, i):
            self.i = i
            self.f = pp[i // 2][:, (i % 2) * 512:(i % 2 + 1) * 512]
            self.b = self.f.bitcast(BF16)
            self.buf = Buf(f"bank{i}")
    bk = [Bank(i) for i in range(8)]

    def pair_f(j):
        return pp[j][:, :]

    def pair_b(j):
        return pp[j][:, :].bitcast(BF16)

    bX = [Buf(f"X{s}") for s in range(4)]
    b_hT = [[Buf(f"hT{kc}_{s}") for s in range(4)] for kc in range(16)]
    b_hT_all = [b for row in b_hT for b in row]
    bW = [Buf(f"Wslot{i}") for i in range(3)]
    b_hn = [Buf("hn0"), Buf("hn1")]
    b_ss = Buf("ss")
    b_rstd = Buf("rstd")
    b_const = Buf("const")
    b_alrT = Buf("alrT")
    b_sp = [Buf("sp0"), Buf("sp1")]
    b_etmp = [Buf("et0"), Buf("et1")]
    b_Etmp = [Buf("E0"), Buf("E1")]
    b_qT = [Buf(f"qT{h}") for h in range(4)]
    b_kT = [Buf(f"kT{h}") for h in range(4)]
    b_ktm = Buf("ktm")
    b_vtm = [Buf(f"vtm{s}") for s in range(4)]
    b_rs = [Buf(f"rs{s}") for s in range(4)]
    b_ebt = Buf("ebt")
    b_sT = [Buf("sT0"), Buf("sT1")]
    b_ygt = [Buf("ygt0"), Buf("ygt1")]
    b_ssqo = [Buf("ssqo0"), Buf("ssqo1")]
    b_rstdo = [Buf("rstdo0"), Buf("rstdo1")]
    b_S = Buf("S")
    b_Sbf = Buf("Sbf")
    b_cbs = [Buf(f"cbs{i}") for i in range(4)]
    b_ccs = [Buf(f"ccs{i}") for i in range(4)]
    b_u = [Buf("u0"), Buf("u1")]
    b_cacc = [Buf("cacc0"), Buf("cacc1")]
    b_uhalo = [Buf(f"uhalo{c}") for c in range(8)]
    b_fhalo = [Buf(f"fhalo{c}") for c in range(88)]
    b_yT = [[Buf(f"yT{c}_{s}") for s in range(4)] for c in range(16)]
    b_ymix = [Buf(f"ymix{s}") for s in range(4)]
    b_hid = [Buf(f"hid{c}") for c in range(44)]
    b_sg = [Buf(f"sg{i}") for i in range(4)]
    b_facc = [Buf("facc0"), Buf("facc1")]
    b_yffn = [Buf(f"yffn{s}") for s in range(4)]
    b_out = [Buf(f"out{s}") for s in range(4)]
    mix_bufs = (b_sp + b_etmp + b_Etmp + b_qT + b_kT + [b_ktm] + b_vtm + b_rs + b_sT + b_ygt + b_cbs + b_ccs
                + b_u + b_cacc + [b for row in b_yT for b in row] + b_ymix)
    ffn_bufs = b_hid + b_sg + b_facc + b_yffn
    tmp_bufs = (b_sp + b_etmp + b_Etmp + b_qT + b_kT + [b_ktm] + b_vtm + b_rs + b_sT + b_ygt + b_cbs + b_ccs
                + b_u + b_cacc)

    sem_W = [S.new_dma_sem(f"semW{i}") for i in range(3)]
    sem_Wp = [S.new_dma_sem(f"semWp{i}") for i in range(3)]
    sem_x = [S.new_dma_sem(f"semx{i}") for i in range(4)]
    sem_o = [S.new_dma_sem(f"semo{i}") for i in range(4)]
    sem_cv = [S.new_dma_sem(f"semcv{i}") for i in range(4)]
    sem_misc = S.new_dma_sem("semmisc")
    sem_misc2 = S.new_dma_sem("semmisc2")
    sem_m3 = S.new_dma_sem("semm3")
    sem_m4 = S.new_dma_sem("semm4")
    sem_m5 = S.new_dma_sem("semm5")

    def mm_group(out_ap, pairs, reads, writes, same_ok=False):
        def fn(e):
            r = None
            n = len(pairs)
            for i, (l, rh) in enumerate(pairs):
                r = e.matmul(out_ap, lhsT=l, rhs=rh, start=(i == 0), stop=(i == n - 1))
            return r
        return S.op("pe", fn, reads=reads, writes=writes, same_ok=same_ok)

    def act(out, in_, func, reads, writes, scale=1.0, bias=None, accum=None):
        kw = {}
        if bias is not None:
            kw["bias"] = bias
        if accum is not None:
            kw["accum_out"] = accum
        return S.op("act", lambda e: e.activation(out=out, in_=in_, func=func, scale=scale, **kw),
                    reads=reads, writes=writes)

    misc_bufs = Buf("vstage")
    S.op("pool", lambda e: e.memset(identf[:], 1.0), writes=[b_const])
    S.op("pool", lambda e: e.affine_select(out=identf[:], in_=identf[:], pattern=[[-1, 128]],
                                           compare_op=ALU.is_equal, fill=0.0, base=0, channel_multiplier=1),
         writes=[b_const])
    S.op("pool", lambda e: e.tensor_copy(out=identb[:], in_=identf[:]), reads=[b_const], writes=[Buf("identb")])
    b_tri = Buf("tri")
    S.op("pool", lambda e: e.memset(tri[:], 1.0), writes=[b_tri])
    S.op("pool", lambda e: e.affine_select(out=tri[:], in_=tri[:], pattern=[[1, 128]],
                                           compare_op=ALU.is_ge, fill=0.0, base=0, channel_multiplier=-1),
         writes=[b_tri])
    S.op("pool", lambda e: e.memset(ones[:], 1.0), writes=[b_const])
    S.op("pool", lambda e: e.memset(epsb[:], EPS), writes=[b_const])
    S.op("pool", lambda e: e.memset(Sst[:], 0.0), writes=[b_S])
    S.op("pool", lambda e: e.memset(Sbf[:], 0.0), writes=[b_Sbf])
    S.op("pool", lambda e: e.memset(uhalo[:], 0.0), writes=b_uhalo)
    S.op("pool", lambda e: e.memset(fhalo[:], 0.0), writes=b_fhalo)
    S.op("pool", lambda e: e.memset(gw2aug[:], 0.0), writes=[b_const])
    S.op("pool", lambda e: e.memset(alrT[:], 1.0), writes=[b_alrT])
    S.dma("sp", sem_m3, lambda e: e.dma_start(out=tm[:], in_=tmask_d), writes=[b_const])
    S.dma("sp", sem_m4, lambda e: e.dma_start(out=gw2aug[0:16, :], in_=gate_w2), writes=[b_const])
    S.dma("sp", sem_m5, lambda e: e.dma_start(out=gw2aug[16:17, :], in_=gate_b), writes=[b_const])
    S.dma("pool", sem_misc2, lambda e: e.dma_start(out=Walr[:], in_=w_in[:, 6144:6160].rearrange("(kc p) n -> p kc n", p=128)),
          writes=[b_const])

    b_vst = Buf("vstage")
    vrot = Rot([bk[6], bk[7]])

    def vec_fm(src, n, dst):
        S.dma("sp", sem_misc, lambda e: e.dma_start(out=vstage[0:n, :], in_=src), writes=[b_vst])
        pb = vrot.next()
        S.op("pe", lambda e: e.transpose(out=pb.f[:, 0:n], in_=vstage[0:n, :], identity=identf[0:n, 0:n]),
             reads=[b_vst, b_const], writes=[pb.buf])
        S.op("dve", lambda e: e.tensor_copy(out=dst, in_=pb.f[:, 0:n]), reads=[pb.buf], writes=[b_const])

    vec_fm(cvec, 16, c_fm[:, :])
    vec_fm(b_mod, 96, bmod_fm[:, :])
    vec_fm(pre_m, 16, prem_fm[:, :])
    vec_fm(post_m, 16, postm_fm[:, :])
    vec_fm(pre_f, 16, pref_fm[:, :])
    vec_fm(post_f, 16, postf_fm[:, :])
    vec_fm(conv_w, 24, convw_fm[:, :])
    vec_fm(gnw, 2, gnw_fm[:, :])
    for t3 in range(3):
        vec_fm(fconv_w[t3 * 88:(t3 + 1) * 88, :], 88, fconvw_fm[:, t3 * 88:(t3 + 1) * 88])
    act(ca_bf[:], c_fm[:], AF.Silu, reads=[b_const], writes=[b_const])

    pmod = bk[5]
    for blk in range(24):
        sl = blk % 3
        S.dma("pool", sem_Wp[sl], lambda e, sl=sl, blk=blk: e.dma_start(
            out=Wsl[sl][:], in_=w_mod[:, blk * 512:(blk + 1) * 512].rearrange("(kc p) n -> p kc n", p=128)),
            writes=[bW[sl]])
        for nch in range(4):
            col = blk * 4 + nch
            mm_group(pmod.f[:, col:col + 1],
                     [(Wsl[sl][:, kc, nch * 128:(nch + 1) * 128], ca_bf[:, kc:kc + 1]) for kc in range(16)],
                     reads=[bW[sl], b_const], writes=[pmod.buf], same_ok=True)
    S.op("dve", lambda e: e.tensor_tensor(out=mod_fm[:], in0=pmod.f[:, 0:96], in1=bmod_fm[:], op=ALU.add),
         reads=[pmod.buf, b_const], writes=[b_const])
    S.op("dve", lambda e: e.scalar_tensor_tensor(out=A_m[:], in0=mod_fm[:, 16:32], scalar=1.0, in1=prem_fm[:],
                                                 op0=ALU.add, op1=ALU.mult), reads=[b_const], writes=[b_const])
    S.op("dve", lambda e: e.scalar_tensor_tensor(out=A_f[:], in0=mod_fm[:, 64:80], scalar=1.0, in1=pref_fm[:],
                                                 op0=ALU.add, op1=ALU.mult), reads=[b_const], writes=[b_const])
    S.op("dve", lambda e: e.tensor_tensor(out=Gm_fm[:], in0=mod_fm[:, 32:48], in1=postm_fm[:], op=ALU.mult),
         reads=[b_const], writes=[b_const])
    S.op("dve", lambda e: e.tensor_tensor(out=Gf_fm[:], in0=mod_fm[:, 80:96], in1=postf_fm[:], op=ALU.mult),
         reads=[b_const], writes=[b_const])
    B_m = mod_fm[:, 0:16]
    B_f = mod_fm[:, 48:64]
    b_dtmp = [Buf("dtmp0"), Buf("dtmp1")]
    grot = Rot([bk[6], bk[7]])
    for (gfm, gdst) in ((Gm_fm, Gm), (Gf_fm, Gf)):
        for g4 in range(4):
            pb = grot.next()
            for j in range(4):
                c = g4 * 4 + j
                dd = c % 2
                S.op("dve", lambda e, c=c, dd=dd, gfm=gfm: e.tensor_scalar(
                    out=dtmp[:, dd, :], in0=identf[:], scalar1=gfm[:, c:c + 1], scalar2=None, op0=ALU.mult),
                    reads=[b_const], writes=[b_dtmp[dd]])
                mm_group(pb.f[:, j * 128:(j + 1) * 128], [(ones[:], dtmp[:, dd, :])],
                         reads=[b_const, b_dtmp[dd]], writes=[pb.buf], same_ok=True)
            S.op("dve", lambda e, pb=pb, g4=g4, gdst=gdst: e.tensor_copy(out=gdst[:, g4 * 512:(g4 + 1) * 512], in_=pb.f),
                 reads=[pb.buf], writes=[b_const])

    cv_toks = []
    scr_buf = {}

    def convert(key, dst, src):
        i = len(cv_toks)
        b = Buf("scr")
        extra = [cv_toks[i - 4]] if i >= 4 else None
        t = S.dma("pool", sem_cv[i % 4], lambda e: e.dma_start(out=dst, in_=src), writes=[b], extra=extra)
        cv_toks.append(t)
        scr_buf[key] = b

    def win_src(blk):
        return w_in[:, blk * 512:(blk + 1) * 512].rearrange("(kc p) n -> p kc n", p=128)

    for blk in (7, 8, 9, 6, 10, 11, 0, 2, 4, 1, 3, 5):
        convert(("win", blk), s_win[blk], win_src(blk))
    for blk in range(4):
        convert(("wout", blk), s_wout[blk], w_out[:, blk * 512:(blk + 1) * 512].rearrange("(kc p) n -> p kc n", p=128))
    for b in range(11):
        for blk in (b, 11 + b):
            convert(("wup", blk), s_wup[blk], w_up[:, blk * 512:(blk + 1) * 512].rearrange("(kc p) n -> p kc n", p=128))
    for cg in range(4):
        for kb in range(3):
            nk = 16 if kb < 2 else 12
            convert(("wdn", cg * 3 + kb), s_wdn[cg * 3 + kb][:, 0:nk, :],
                    w_down[kb * 2048:kb * 2048 + nk * 128, cg * 512:(cg + 1) * 512].rearrange("(kc p) n -> p kc n", p=128))

    tiles = []
    for i in range(NPRE):
        ns = 4 if i < NPRE - 1 else 3
        tiles.append(("pre", xpre[i * 512:i * 512 + ns * 128, :], ns, i, None))
    tiles.append(("warm", xpre[NPRE * 512 - 128:NPRE * 512, :], 1, NPRE, None))
    for i in range(NMAIN):
        tiles.append(("main", xmain[i * 512:(i + 1) * 512, :], 4, NPRE + 1 + i, out_d[i * 512:(i + 1) * 512, :]))

    plan = []
    for ti_, (kind, _, _, _, _) in enumerate(tiles):
        if kind == "pre":
            seq = [("win", 7), ("win", 8), ("win", 9)] if ti_ == 0 else []
        else:
            seq = [("win", b) for b in (7, 6, 8, 9, 10, 11, 0, 2, 4, 1, 3, 5)]
            seq += [("wout", b) for b in range(4)]
            for b in range(11):
                seq += [("wup", b), ("wup", 11 + b)]
            if kind == "main":
                seq += [("wdn", i) for i in range(12)]
        plan.extend(seq)
    scr_ap = {"win": s_win, "wout": s_wout, "wup": s_wup, "wdn": s_wdn}
    wst = {"issued": 0, "next": 0, "limit": 3 if NPRE > 0 else len(plan)}

    def w_issue_upto(n):
        while wst["issued"] < min(n, len(plan), wst["limit"]):
            i = wst["issued"]
            sl = i % 3
            key = plan[i]
            nk = 12 if (key[0] == "wdn" and key[1] % 3 == 2) else 16
            src = scr_ap[key[0]][key[1]][:, 0:nk, :]
            S.dma("sp", sem_W[sl], lambda e, sl=sl, nk=nk, src=src: e.dma_start(out=Wsl[sl][:, 0:nk, :], in_=src),
                  reads=[scr_buf[key]], writes=[bW[sl]])
            wst["issued"] += 1

    def w_get(expect, resident=None):
        if resident is not None:
            return Wsl[resident], bW[resident]
        i = wst["next"]
        assert plan[i] == expect, (i, plan[i], expect)
        w_issue_upto(i + 3)
        wst["next"] += 1
        return Wsl[i % 3], bW[i % 3]

    allbanks = Rot(bk)

    b_ssl = [Buf(f"ss{s}") for s in range(4)]
    b_rstdl = [Buf(f"rstd{s}") for s in range(4)]

    def stage_in_A1(xsrc, ns, from_X):
        for s in range(ns):
            if not from_X:
                S.dma("sp", sem_x[s], lambda e, s=s: e.dma_start(out=X[:, s, :], in_=xsrc[s * 128:(s + 1) * 128, :]),
                      writes=[bX[s]])
        for s in range(ns):
            act(hn[:, (s + 1) % 2, :], X[:, s, :], AF.Square, reads=[bX[s]], writes=[b_hn[(s + 1) % 2], b_ssl[s]],
                accum=ssb[:, s:s + 1])
            act(rstd[:, s:s + 1], ssb[:, s:s + 1], AF.Ln, reads=[b_ssl[s]], writes=[b_rstdl[s]], scale=1.0 / D, bias=epsb[:])
            act(rstd[:, s:s + 1], rstd[:, s:s + 1], AF.Exp, reads=[b_rstdl[s]], writes=[b_rstdl[s]], scale=-0.5)

    def stage_in_A2(ns, A, B):
        trot = Rot([bk[0], bk[1], bk[2], bk[3]])

        def emit_hn(s):
            hb = s % 2
            S.op("dve", lambda e, s=s, hb=hb: e.tensor_scalar(out=hn[:, hb, :], in0=X[:, s, :], scalar1=rstd[:, s:s + 1],
                                                             scalar2=None, op0=ALU.mult),
                 reads=[bX[s], b_rstdl[s]], writes=[b_hn[hb]])
        emit_hn(0)
        for s in range(ns):
            hb = s % 2
            if s + 1 < ns:
                emit_hn(s + 1)
            for g in range(4):
                pb = trot.next()
                pv = pb.b[:, 0:512].rearrange("p (a b) -> p a b", a=4)
                for j in range(4):
                    kc = g * 4 + j
                    S.op("pe", lambda e, kc=kc, j=j, hb=hb, pv=pv: e.transpose(
                        out=pv[:, j, :], in_=hn[:, hb, kc * 128:(kc + 1) * 128], identity=identb[:]),
                        reads=[b_hn[hb]], writes=[pb.buf], same_ok=True)
                for j in range(4):
                    kc = g * 4 + j
                    if g % 2 == 0:
                        S.op("dve", lambda e, kc=kc, j=j, s=s, pv=pv: e.tensor_scalar(
                            out=hT[:, kc, s * 128:(s + 1) * 128], in0=pv[:, j, :], scalar1=A[:, kc:kc + 1],
                            scalar2=B[:, kc:kc + 1], op0=ALU.mult, op1=ALU.add),
                            reads=[pb.buf, b_const], writes=[b_hT[kc][s]])
                    else:
                        act(hT[:, kc, s * 128:(s + 1) * 128], pv[:, j, :], AF.Identity, reads=[pb.buf, b_const],
                            writes=[b_hT[kc][s]], scale=A[:, kc:kc + 1], bias=B[:, kc:kc + 1])

    def stage_in(xsrc, ns, A, B, from_X):
        stage_in_A1(xsrc, ns, from_X)
        stage_in_A2(ns, A, B)

    def hT_reads(ns):
        return [b_hT[kc][s] for kc in range(16) for s in range(ns)]

    def fm_block(Wt, Wb, ns, i):
        N = ns * 128
        pb = allbanks.next()
        mm_group(pb.f[:, 0:N], [(Wt[:, kc, i * 128:(i + 1) * 128], hT[:, kc, 0:N]) for kc in range(16)],
                 reads=[Wb] + hT_reads(ns), writes=[pb.buf])
        return pb

    def tm_block(Wt, Wb, s, src, src_bufs, nk=16, kofs=0):
        pb = allbanks.next()
        mm_group(pb.f, [(src[:, kofs + kc, s * 128:(s + 1) * 128], Wt[:, kc, :]) for kc in range(nk)],
                 reads=[Wb] + src_bufs, writes=[pb.buf])
        return pb

    def do_stage_A1(tile):
        alias_fence(ffn_bufs, mix_bufs)
        if tile[3] == 0:
            alias_fence([b_vst] + b_dtmp, mix_bufs)
        stage_in_A1(tile[1], tile[2], from_X=False)

    def do_stage_A2(tile):
        stage_in_A2(tile[2], A_m, B_m)

    def do_stage_A(tile):
        do_stage_A1(tile)
        do_stage_A2(tile)

    staged = set()
    for ti, (kind, xsrc, ns, mcol, odst) in enumerate(tiles):
        N = ns * 128
        full = kind != "pre"
        tmc = tm[:, mcol:mcol + 1]
        res_w = (kind == "pre" and ti > 0)
        if ti not in staged:
            do_stage_A(tiles[ti])

        pa = bk[4]
        mm_group(pa.f[0:16, 0:N], [(Walr[:, kc, :], hT[:, kc, 0:N]) for kc in range(16)],
                 reads=[b_const] + hT_reads(ns), writes=[pa.buf])
        act(alrT[0:16, 0:N], pa.f[0:16, 0:N], AF.Copy, reads=[pa.buf], writes=[b_alrT])
        pnb = [bk[0], bk[1], bk[2], bk[3]]
        zrot = Rot([bk[5], bk[6]])
        for s in range(ns):
            pz = zrot.next()
            mm_group(pz.f, [(alrT[0:32, s * 128:(s + 1) * 128], gw2aug[0:32, :])], reads=[b_alrT, b_const], writes=[pz.buf])
            sb_ = s % 2
            act(etmp[:, sb_, :], pz.f, AF.Exp, reads=[pz.buf], writes=[b_etmp[sb_]], scale=-1.0)
            act(sp[:, sb_, :], etmp[:, sb_, :], AF.Ln, reads=[b_etmp[sb_], b_const], writes=[b_sp[sb_]], scale=1.0, bias=ones[:, 0:1])
            for h in range(4):
                mm_group(pnb[h].f[:, s * 128:(s + 1) * 128], [(sp[:, sb_, h * 128:(h + 1) * 128], tri[:])],
                         reads=[b_sp[sb_], b_tri], writes=[pnb[h].buf], same_ok=True)
        for h in range(4):
            act(ebt[:, h, 0:ns], pnb[h].f[:, 127:N:128], AF.Exp, reads=[pnb[h].buf], writes=[b_ebt], scale=-1.0 / 16)

        Wt, Wb = w_get(("win", 7), resident=0 if res_w else None)
        kq_rot = Rot([bk[4], bk[5]])
        pktv = pair_b(3).rearrange("p (s h d) -> p s h d", s=4, h=4)
        pkt_bufs = [bk[6].buf, bk[7].buf]
        for h in range(4):
            pb = kq_rot.next()
            mm_group(pb.f[:, 0:N], [(Wt[:, kc, h * 128:(h + 1) * 128], hT[:, kc, 0:N]) for kc in range(16)],
                     reads=[Wb] + hT_reads(ns), writes=[pb.buf])
            eb_ = h % 2
            act(Etmp[:, eb_, 0:N], pnb[h].f[:, 0:N], AF.Exp, reads=[pnb[h].buf], writes=[b_Etmp[eb_]], scale=1.0 / 16)
            S.op("dve", lambda e, h=h, pb=pb, eb_=eb_, N=N: e.tensor_tensor(out=kT[:, h, 0:N], in0=pb.f[:, 0:N], in1=Etmp[:, eb_, 0:N],
                                                                    op=ALU.mult),
                 reads=[pb.buf, b_Etmp[eb_]], writes=[b_kT[h]])
            for s in range(ns):
                S.op("pe", lambda e, h=h, s=s: e.transpose(out=pktv[:, s, h, :], in_=kT[:, h, s * 128:(s + 1) * 128],
                                                          identity=identb[:]),
                     reads=[b_kT[h]], writes=pkt_bufs, same_ok=True)
        S.op("act", lambda e, ns=ns: e.activation(out=ktm[:, 0:ns, :], in_=pair_b(3)[:, 0:ns * 512].rearrange("p (s n) -> p s n", s=ns),
                                           func=AF.Copy),
             reads=pkt_bufs, writes=[b_ktm])

        if full:
            Wt, Wb = w_get(("win", 6))
            for h in range(4):
                pb = kq_rot.next()
                mm_group(pb.f[:, 0:N], [(Wt[:, kc, h * 128:(h + 1) * 128], hT[:, kc, 0:N]) for kc in range(16)],
                         reads=[Wb] + hT_reads(ns), writes=[pb.buf])
                eb_ = h % 2
                act(Etmp[:, eb_, 0:N], pnb[h].f[:, 0:N], AF.Exp, reads=[pnb[h].buf], writes=[b_Etmp[eb_]], scale=-1.0 / 16)
                S.op("dve", lambda e, h=h, pb=pb, eb_=eb_, N=N: e.scalar_tensor_tensor(
                    out=qT[:, h, 0:N], in0=pb.f[:, 0:N], scalar=128.0 ** -0.5, in1=Etmp[:, eb_, 0:N],
                    op0=ALU.mult, op1=ALU.mult),
                    reads=[pb.buf, b_Etmp[eb_]], writes=[b_qT[h]])

        if (not full) and ti + 1 < len(tiles):
            do_stage_A1(tiles[ti + 1])
        for half in range(2):
            Wt, Wb = w_get(("win", 8 + half), resident=(1 + half) if res_w else None)
            for s in range(ns):
                pb = tm_block(Wt, Wb, s, hT, [b_hT[kc][s] for kc in range(16)])
                act(vtm[:, s, half * 512:(half + 1) * 512], pb.f, AF.Identity, reads=[pb.buf, b_const], writes=[b_vtm[s]],
                    scale=tmc)
        if full:
            for half in range(2):
                Wt, Wb = w_get(("win", 10 + half))
                for s in range(ns):
                    pb = tm_block(Wt, Wb, s, hT, [b_hT[kc][s] for kc in range(16)])
                    act(rs[:, s, half * 512:(half + 1) * 512], pb.f, AF.Silu, reads=[pb.buf], writes=[b_rs[s]])

        sc_rot = Rot([bk[4], bk[5]])
        yt_rot = Rot([bk[6], bk[7]])

        def gla_a(s):
            pb = sc_rot.next()
            pv = pb.f.rearrange("p (h i) -> p h i", h=4)
            for h in range(4):
                mm_group(pv[:, h, :], [(kT[:, h, s * 128:(s + 1) * 128], qT[:, h, s * 128:(s + 1) * 128])],
                         reads=[b_kT[h], b_qT[h]], writes=[pb.buf], same_ok=True)
            par = s % 2
            S.op("dve", lambda e: e.tensor_tensor(out=sTb[:, par, :, :], in0=pv,
                                                  in1=tri[:, :].unsqueeze(1).to_broadcast([128, 4, 128]), op=ALU.mult),
                 reads=[pb.buf, b_tri], writes=[b_sT[par]])

        def gla_state(s, alt=False):
            pj = 1 + (s % 2) if alt else 1
            pds = pair_f(pj)
            pds_bufs = [bk[2 * pj].buf, bk[2 * pj + 1].buf]
            for h in range(4):
                mm_group(pds[:, h * 256:(h + 1) * 256], [(ktm[:, s, h * 128:(h + 1) * 128], vtm[:, s, h * 256:(h + 1) * 256])],
                         reads=[b_ktm, b_vtm[s]], writes=pds_bufs, same_ok=True)
            Sflat = Sst[:, :, :].rearrange("p h e -> p (h e)")
            S.op("dve", lambda e: e.tensor_tensor(out=Sflat, in0=pds, in1=Sflat, op=ALU.add),
                 reads=pds_bufs, writes=[b_S])
            S.op("dve", lambda e: e.tensor_tensor(out=Sst[:, :, :], in0=Sst[:, :, :],
                                                  in1=ebt[:, :, s:s + 1].to_broadcast([128, 4, 256]), op=ALU.mult),
                 reads=[b_ebt], writes=[b_S])
            act(Sbf[:, :, :], Sst[:, :, :], AF.Copy, reads=[b_S], writes=[b_Sbf])

        def gla_b(s):
            par = s % 2
            po = pair_f(0)
            po_bufs = [bk[0].buf, bk[1].buf]
            for h in range(4):
                mm_group(po[:, h * 256:(h + 1) * 256],
                         [(sTb[:, par, h, :], vtm[:, s, h * 256:(h + 1) * 256]),
                          (qT[:, h, s * 128:(s + 1) * 128], Sbf[:, h, :])],
                         reads=[b_sT[par], b_vtm[s], b_qT[h], b_Sbf], writes=po_bufs, same_ok=True)
            gla_state(s)
            for h in range(4):
                act(jk[:, 0:256], po[:, h * 256:(h + 1) * 256], AF.Square, reads=po_bufs, writes=[b_ssqo[par]],
                    accum=ssq_o[:, par, h:h + 1])
            act(rstd_o[:, par, :], ssq_o[:, par, :], AF.Ln, reads=[b_ssqo[par]], writes=[b_rstdo[par]], scale=1.0 / 256,
                bias=epsb[:])
            act(rstd_o[:, par, :], rstd_o[:, par, :], AF.Exp, reads=[b_rstdo[par]], writes=[b_rstdo[par]], scale=-0.5)
            for h in range(4):
                S.op("dve", lambda e, h=h: e.scalar_tensor_tensor(
                    out=ygt[:, par, h * 256:(h + 1) * 256], in0=po[:, h * 256:(h + 1) * 256], scalar=rstd_o[:, par, h:h + 1],
                    in1=rs[:, s, h * 256:(h + 1) * 256], op0=ALU.mult, op1=ALU.mult),
                    reads=po_bufs + [b_rstdo[par], b_rs[s]], writes=[b_ygt[par]])

        def gla_c(s):
            par = s % 2
            pb = yt_rot.next()
            pv = pb.b.rearrange("p (c t) -> p c t", c=8)
            for c in range(8):
                S.op("pe", lambda e, c=c: e.transpose(out=pv[:, c, :], in_=ygt[:, par, c * 128:(c + 1) * 128], identity=identb[:]),
                     reads=[b_ygt[par]], writes=[pb.buf], same_ok=True)
            for ec in range(2):
                S.op("dve", lambda e, ec=ec: e.tensor_scalar(
                    out=yT[:, 8 + ec:16:2, s * 128:(s + 1) * 128], in0=pv[:, ec:8:2, :], scalar1=gnw_fm[:, ec:ec + 1],
                    scalar2=None, op0=ALU.mult),
                    reads=[pb.buf, b_const], writes=[b_yT[8 + 2 * h + ec][s] for h in range(4)])

        if not full:
            if ti + 1 < len(tiles) and tiles[ti + 1][0] != "pre":
                wst["limit"] = len(plan)
                w_issue_upto(wst["next"] + 2)
            if ti + 1 < len(tiles):
                do_stage_A2(tiles[ti + 1])
                staged.add(ti + 1)
            for s in range(ns):
                gla_state(s, alt=True)
            continue

        def conv_copy_block(key, dst, dbufs):
            def run():
                Wt, Wb = w_get(key)
                for i in range(4):
                    pb = fm_block(Wt, Wb, ns, i)
                    act(dst[:, i, 0:N], pb.f[:, 0:N], AF.Copy, reads=[pb.buf], writes=[dbufs[i]])
            return run

        def conv_x_block(hf):
            def run():
                Wt, Wb = w_get(("win", 4 + hf))
                for i in range(4):
                    c = hf * 4 + i
                    ub = i % 2
                    pb = fm_block(Wt, Wb, ns, i)
                    S.op("dve", lambda e, ub=ub, c=c: e.tensor_copy(out=ubuf[:, ub, 0:2], in_=uhalo[:, c, :]),
                         reads=[b_uhalo[c]], writes=[b_u[ub]])
                    S.op("dve", lambda e, ub=ub, i=i, pb=pb, N=N: e.tensor_tensor(out=ubuf[:, ub, 2:2 + N], in0=pb.f[:, 0:N],
                                                                          in1=ccs[:, i, 0:N], op=ALU.mult),
                         reads=[pb.buf, b_ccs[i]], writes=[b_u[ub]])
                    S.op("dve", lambda e, ub=ub, c=c, N=N, tmc=tmc: e.tensor_scalar(out=uhalo[:, c, :], in0=ubuf[:, ub, N:N + 2], scalar1=tmc,
                                                                     scalar2=None, op0=ALU.mult),
                         reads=[b_u[ub], b_const], writes=[b_uhalo[c]])
                    act(cacc[:, ub, 0:N], ubuf[:, ub, 2:2 + N], AF.Identity, reads=[b_u[ub], b_const], writes=[b_cacc[ub]],
                        scale=convw_fm[:, 16 + c:17 + c])
                    S.op("dve", lambda e, ub=ub, c=c, N=N: e.scalar_tensor_tensor(
                        out=cacc[:, ub, 0:N], in0=ubuf[:, ub, 1:1 + N], scalar=convw_fm[:, 8 + c:9 + c], in1=cacc[:, ub, 0:N],
                        op0=ALU.mult, op1=ALU.add), reads=[b_u[ub], b_const], writes=[b_cacc[ub]])
                    S.op("dve", lambda e, ub=ub, c=c, N=N: e.scalar_tensor_tensor(
                        out=cacc[:, ub, 0:N], in0=ubuf[:, ub, 0:N], scalar=convw_fm[:, c:c + 1], in1=cacc[:, ub, 0:N],
                        op0=ALU.mult, op1=ALU.add), reads=[b_u[ub], b_const], writes=[b_cacc[ub]])
                    S.op("pool", lambda e, ub=ub, c=c, i=i, N=N: e.tensor_tensor(out=yT[:, c, 0:N], in0=cacc[:, ub, 0:N],
                                                                           in1=cbs[:, i, 0:N], op=ALU.mult),
                         reads=[b_cacc[ub], b_cbs[i]], writes=[b_yT[c][s] for s in range(ns)])
            return run

        conv_steps = []
        for hf in range(2):
            conv_steps.append(conv_copy_block(("win", 0 + hf), cbs, b_cbs))
            conv_steps.append(conv_copy_block(("win", 2 + hf), ccs, b_ccs))
            conv_steps.append(conv_x_block(hf))
        gla_steps = [lambda: gla_a(0), lambda: gla_b(0)]
        for s in range(1, ns):
            gla_steps.append(lambda s=s: gla_a(s))
            gla_steps.append(lambda s=s: gla_c(s - 1))
            gla_steps.append(lambda s=s: gla_b(s))
        gla_steps.append(lambda: gla_c(ns - 1))
        gi = 0
        for ci, cstep in enumerate(conv_steps):
            for _ in range(2):
                if gi < len(gla_steps):
                    gla_steps[gi]()
                    gi += 1
            cstep()
        while gi < len(gla_steps):
            gla_steps[gi]()
            gi += 1

        alias_fence(tmp_bufs, b_ymix)
        for cg in range(4):
            Wt, Wb = w_get(("wout", cg))
            for s in range(ns):
                pb = tm_block(Wt, Wb, s, yT, [b_yT[c][s] for c in range(16)])
                act(ymix[:, s, cg * 512:(cg + 1) * 512], pb.f, AF.Copy, reads=[pb.buf], writes=[b_ymix[s]])
        for s in range(ns):
            act(hn[:, s % 2, :], ymix[:, s, :], AF.Square, reads=[b_ymix[s]], writes=[b_hn[s % 2], b_ssl[s]], accum=ssb[:, s:s + 1])
        act(rstd[:, 0:ns], ssb[:, 0:ns], AF.Ln, reads=b_ssl[0:ns], writes=b_rstdl[0:ns], scale=1.0 / D, bias=epsb[:])
        act(rstd[:, 0:ns], rstd[:, 0:ns], AF.Exp, reads=b_rstdl[0:ns], writes=b_rstdl[0:ns], scale=-0.5)
        for s in range(ns):
            S.op("dve", lambda e, s=s: e.scalar_tensor_tensor(out=ymix[:, s, :], in0=ymix[:, s, :], scalar=rstd[:, s:s + 1],
                                                             in1=Gm[:, :], op0=ALU.mult, op1=ALU.mult),
                 reads=[b_rstdl[s], b_const], writes=[b_ymix[s]])
            S.op("pool", lambda e, s=s: e.tensor_tensor(out=X[:, s, :], in0=X[:, s, :], in1=ymix[:, s, :], op=ALU.add),
                 reads=[b_ymix[s]], writes=[bX[s]])

        stage_in(None, ns, A_f, B_f, from_X=True)
        alias_fence(mix_bufs, ffn_bufs)
        for b in range(11):
            for part in range(2):
                blk = b + 11 * part
                Wt, Wb = w_get(("wup", blk))
                for i in range(4):
                    gc = 4 * b + i
                    ch = gc + 44 * part
                    fb = i % 2
                    w0 = fconvw_fm[:, ch:ch + 1]
                    w1 = fconvw_fm[:, 88 + ch:89 + ch]
                    w2 = fconvw_fm[:, 176 + ch:177 + ch]
                    pb = fm_block(Wt, Wb, ns, i)
                    act(facc[:, fb, 0:N], pb.f[:, 0:N], AF.Identity, reads=[pb.buf, b_const], writes=[b_facc[fb]], scale=w2)
                    S.op("dve", lambda e, fb=fb, pb=pb, w1=w1, N=N: e.scalar_tensor_tensor(
                        out=facc[:, fb, 1:N], in0=pb.f[:, 0:N - 1], scalar=w1, in1=facc[:, fb, 1:N], op0=ALU.mult, op1=ALU.add),
                        reads=[pb.buf, b_const], writes=[b_facc[fb]])
                    S.op("dve", lambda e, fb=fb, pb=pb, w0=w0, N=N: e.scalar_tensor_tensor(
                        out=facc[:, fb, 2:N], in0=pb.f[:, 0:N - 2], scalar=w0, in1=facc[:, fb, 2:N], op0=ALU.mult, op1=ALU.add),
                        reads=[pb.buf, b_const], writes=[b_facc[fb]])
                    S.op("dve", lambda e, fb=fb, ch=ch, w1=w1: e.scalar_tensor_tensor(
                        out=facc[:, fb, 0:1], in0=fhalo[:, ch, 1:2], scalar=w1, in1=facc[:, fb, 0:1], op0=ALU.mult, op1=ALU.add),
                        reads=[b_fhalo[ch], b_const], writes=[b_facc[fb]])
                    S.op("dve", lambda e, fb=fb, ch=ch, w0=w0: e.scalar_tensor_tensor(
                        out=facc[:, fb, 0:2], in0=fhalo[:, ch, 0:2], scalar=w0, in1=facc[:, fb, 0:2], op0=ALU.mult, op1=ALU.add),
                        reads=[b_fhalo[ch], b_const], writes=[b_facc[fb]])
                    act(fhalo[:, ch, :], pb.f[:, N - 2:N], AF.Identity, reads=[pb.buf, b_const], writes=[b_fhalo[ch]], scale=tmc)
                    if part == 0:
                        act(sg[:, i, 0:N], facc[:, fb, 0:N], AF.Silu, reads=[b_facc[fb]], writes=[b_sg[i]])
                    elif kind == "main":
                        S.op("pool", lambda e, fb=fb, gc=gc, i=i, N=N: e.tensor_tensor(out=hid[:, gc, 0:N], in0=facc[:, fb, 0:N],
                                                                                 in1=sg[:, i, 0:N], op=ALU.mult),
                             reads=[b_facc[fb], b_sg[i]], writes=[b_hid[gc]])
        if kind != "main":
            if dbg:
                d_u = nc.dram_tensor("dbg_uhalo", [128, 16], F32, kind="ExternalOutput").ap()
                d_f = nc.dram_tensor("dbg_fhalo", [128, 176], F32, kind="ExternalOutput").ap()
                d_x = nc.dram_tensor("dbg_x1", [128, D], F32, kind="ExternalOutput").ap()
                sem_dbg = [S.new_dma_sem(f"semdbg{i}") for i in range(3)]
                dbg_toks = [
                    S.dma("sp", sem_dbg[0], lambda e: e.dma_start(out=d_u, in_=uhalo[:].rearrange("p a b -> p (a b)")), reads=b_uhalo),
                    S.dma("sp", sem_dbg[1], lambda e: e.dma_start(out=d_f, in_=fhalo[:].rearrange("p a b -> p (a b)")), reads=b_fhalo),
                    S.dma("sp", sem_dbg[2], lambda e: e.dma_start(out=d_x, in_=X[:, 0, :]), reads=[bX[0]])]
                S.final_wait("sp", dbg_toks)
            continue
        alias_fence(b_sg + b_facc, b_yffn)
        for cg in range(4):
            bset = [bk[(cg % 2) * 4 + s] for s in range(4)]
            for kb in range(3):
                nk = 16 if kb < 2 else 12
                Wt, Wb = w_get(("wdn", cg * 3 + kb))
                for s in range(ns):
                    pb = bset[s]

                    def fn(e, pb=pb, s=s, kb=kb, nk=nk, Wt=Wt):
                        r = None
                        for kc in range(nk):
                            r = e.matmul(pb.f, lhsT=hid[:, kb * 16 + kc, s * 128:(s + 1) * 128], rhs=Wt[:, kc, :],
                                         start=(kb == 0 and kc == 0), stop=(kb == 2 and kc == nk - 1))
                        return r
                    S.op("pe", fn, reads=[Wb] + b_hid[kb * 16:kb * 16 + nk], writes=[pb.buf], same_ok=True)
            for s in range(ns):
                act(yffn[:, s, cg * 512:(cg + 1) * 512], bset[s].f, AF.Copy, reads=[bset[s].buf], writes=[b_yffn[s]])
        for s in range(ns):
            act(hn[:, s % 2, :], yffn[:, s, :], AF.Square, reads=[b_yffn[s]], writes=[b_hn[s % 2], b_ssl[s]], accum=ssb[:, s:s + 1])
        act(rstd[:, 0:ns], ssb[:, 0:ns], AF.Ln, reads=b_ssl[0:ns], writes=b_rstdl[0:ns], scale=1.0 / D, bias=epsb[:])
        act(rstd[:, 0:ns], rstd[:, 0:ns], AF.Exp, reads=b_rstdl[0:ns], writes=b_rstdl[0:ns], scale=-0.5)
        for s in range(ns):
            S.op("dve", lambda e, s=s: e.scalar_tensor_tensor(out=yffn[:, s, :], in0=yffn[:, s, :], scalar=rstd[:, s:s + 1],
                                                             in1=Gf[:, :], op0=ALU.mult, op1=ALU.mult),
                 reads=[b_rstdl[s], b_const], writes=[b_yffn[s]])
            S.op("pool", lambda e, s=s: e.tensor_tensor(out=yffn[:, s, :], in0=X[:, s, :], in1=yffn[:, s, :], op=ALU.add),
                 reads=[bX[s]], writes=[b_yffn[s]])
            S.dma("sp", sem_o[s], lambda e, s=s, odst=odst: e.dma_start(out=odst[s * 128:(s + 1) * 128, :], in_=yffn[:, s, :]),
                  reads=[b_yffn[s]], writes=[b_out[s]])

    assert wst["next"] == len(plan), (wst["next"], len(plan))
    S.final_wait("sp", [b.w for b in b_out if b.w is not None])
    S.emit()
    return nc, S


def make_in_maps(x, c, w_mod, b_mod, mix_pre_w, mix_post_w, w_in, conv_w, gate_w2, gate_b, gla_norm_w, w_out,
                 ffn_pre_w, ffn_post_w, w_up, ffn_conv_w, w_down, NMAIN, NPRE, n_cores=8):
    B, SEQ, _ = x.shape
    per = n_cores // B
    seg = NMAIN * 512
    assert per * seg == SEQ
    NT = NPRE + 1 + NMAIN
    f = lambda a: np.ascontiguousarray(np.asarray(a, dtype=np.float32))
    shared = {
        "w_mod": f(w_mod[0]), "b_mod": f(b_mod[0]).reshape(96, 128), "pre_m": f(mix_pre_w[0]).reshape(16, 128),
        "post_m": f(mix_post_w[0]).reshape(16, 128), "w_in": f(w_in[0]), "conv_w": f(conv_w[0]).reshape(24, 128),
        "gate_w2": f(gate_w2[0]), "gate_b": f(gate_b[0]).reshape(1, 512), "gnw": f(gla_norm_w[0]).reshape(2, 128),
        "w_out": f(w_out[0]), "pre_f": f(ffn_pre_w[0]).reshape(16, 128), "post_f": f(ffn_post_w[0]).reshape(16, 128),
        "w_up": f(w_up[0]), "fconv_w": f(ffn_conv_w[0]).reshape(264, 128), "w_down": f(w_down[0]),
    }
    in_maps = []
    for core in range(n_cores):
        b, j = core // per, core % per
        t0 = j * seg
        xm = f(x[b, t0:t0 + seg])
        xp = np.zeros((NPRE * 512, D), np.float32)
        lo = t0 - NPRE * 512
        if lo >= 0:
            xp[:] = x[b, lo:t0]
        elif t0 > 0:
            xp[-t0:] = x[b, 0:t0]
        tmask = np.zeros((128, NT), np.float32)
        for i in range(NPRE):
            if lo + i * 512 >= 0:
                tmask[:, i] = 1.0
        tmask[:, NPRE] = 1.0 if t0 > 0 else 0.0
        tmask[:, NPRE + 1:] = 1.0
        m = dict(shared)
        m.update({"xmain": xm, "xpre": xp, "tmask": tmask, "cvec": f(c[b]).reshape(16, 128)})
        in_maps.append(m)
    return in_maps


_CACHE = {}


def kernel(x, c, w_mod, b_mod, mix_pre_w, mix_post_w, w_in, conv_w, gate_w2, gate_b, gla_norm_w, w_out,
           ffn_pre_w, ffn_post_w, w_up, ffn_conv_w, w_down):
    x = np.asarray(x)
    B, SEQ, _ = x.shape
    n_cores = 8
    per = n_cores // B
    NMAIN = SEQ // per // 512
    NPRE = (per - 1) * NMAIN
    key = (NMAIN, NPRE)
    if key not in _CACHE:
        _CACHE[key] = build_program(NMAIN, NPRE)[0]
    nc = _CACHE[key]
    in_maps = make_in_maps(x, c, w_mod, b_mod, mix_pre_w, mix_post_w, w_in, conv_w, gate_w2, gate_b, gla_norm_w, w_out,
                           ffn_pre_w, ffn_post_w, w_up, ffn_conv_w, w_down, NMAIN, NPRE, n_cores)
    res = run_bass_kernel_spmd(nc, in_maps, core_ids=list(range(n_cores)))
    out = np.empty((B, SEQ, D), np.float32)
    seg = NMAIN * 512
    for core in range(n_cores):
        b, j = core // per, core % per
        out[b, j * seg:(j + 1) * seg] = res.results[core]["out"]
    return out
```
